# Optimizing a Trainium2 kernel written in Bass

```python
import math
import jax, jax.numpy as jnp
from jax import lax
import numpy as np

D_MODEL = 1024
BATCH = 8
SEQ = 4096
DEPTH = 4

CTX_LEN = 256
GRID_W = 64
A_WIDTH = 512
A_GROUPS = 4
CHUNK = 128
B_HEADS = 8
B_QK_DIM = 64
B_V_DIM = 2 * B_QK_DIM
B_WIDTH = B_HEADS * B_V_DIM
QK_COLS = 2 * B_HEADS * B_QK_DIM
Q_BLOCK = 128
ROPE_THETA = 10000.0
C_WIDTH = 512
CONV_W = 3
N_BRANCH = 3
D_FF = 2816
EPS = 1e-6
A_U = 0
A_V = A_U + A_WIDTH
B_Q = A_V + A_WIDTH
B_K = B_Q + QK_COLS
B_V = B_K + QK_COLS
C_IN = B_V + B_WIDTH
GATE = C_IN + 3 * C_WIDTH
IN_COLS = GATE + N_BRANCH * D_MODEL

kernel_name = 'hybrid_sgu_diffattn_shortconv_dit'


def rmsnorm(x, g):
    xf = x.astype(jnp.float32)
    y = xf * lax.rsqrt(jnp.mean(xf * xf, axis=-1, keepdims=True) + EPS)
    return (y * g.astype(jnp.float32)).astype(x.dtype)


def layernorm(x, g):
    xf = x.astype(jnp.float32)
    mu = jnp.mean(xf, axis=-1, keepdims=True)
    var = jnp.mean(jnp.square(xf - mu), axis=-1, keepdims=True)
    return ((xf - mu) * lax.rsqrt(var + EPS) * g.astype(jnp.float32)).astype(x.dtype)


def modulate(h, shift, scale):
    return h * (1.0 + scale) + shift


def dwconv3(x, w):
    xp = jnp.pad(x, ((0, 0), (1, 1), (0, 0)))
    return xp[:, :-2] * w[0] + xp[:, 1:-1] * w[1] + xp[:, 2:] * w[2]


def axial_rope_tables(n_tokens):
    rows = n_tokens // GRID_W
    row = jnp.repeat(jnp.arange(rows, dtype=jnp.float32), GRID_W)
    col = jnp.tile(jnp.arange(GRID_W, dtype=jnp.float32), rows)
    n_freq = B_QK_DIM // 4
    inv = ROPE_THETA ** (-jnp.arange(n_freq, dtype=jnp.float32) / n_freq)
    ang = jnp.stack([row[:, None] * inv, col[:, None] * inv], axis=1)
    ang = ang[None, :, None, None]
    return jnp.cos(ang), jnp.sin(ang)


def apply_rope(t, cos, sin):
    tr = t.astype(jnp.float32).reshape(*t.shape[:-1], 2, 2, B_QK_DIM // 4)
    x1, x2 = tr[..., 0, :], tr[..., 1, :]
    out = jnp.stack([x1 * cos - x2 * sin, x1 * sin + x2 * cos], axis=-2)
    return out.reshape(t.shape).astype(t.dtype)


def qk_heads(t, g):
    bn, tn, _ = t.shape
    return rmsnorm(t.reshape(bn, tn, 2, B_HEADS, B_QK_DIM), g)


def v_heads(t):
    bn, tn, _ = t.shape
    return t.reshape(bn, tn, B_HEADS, B_V_DIM)


def diff_attention(q, k, v, lam):
    s = jnp.einsum('bqahd,bkahd->bahqk', q, k).astype(jnp.float32) * (B_QK_DIM ** -0.5)
    p = jax.nn.softmax(s, axis=-1)
    a = (p[:, 0] - lam * p[:, 1]).astype(v.dtype)
    return jnp.einsum('bhqk,bkhe->bqhe', a, v)


def blocked_diff_attention(q, k, v, lam):
    bn, tn = q.shape[:2]
    nblk = tn // Q_BLOCK
    qb = jnp.moveaxis(q.reshape(bn, nblk, Q_BLOCK, *q.shape[2:]), 1, 0)
    ob = lax.map(lambda qi: diff_attention(qi, k, v, lam), qb)
    return jnp.moveaxis(ob, 0, 1).reshape(bn, tn, B_HEADS, B_V_DIM)


def spatial_gating(u, v, ln_g, w_s, b_s):
    u = jax.nn.gelu(u)
    v = layernorm(jax.nn.gelu(v), ln_g)
    bn, tn, _ = v.shape
    vc = v.reshape(bn, tn // CHUNK, CHUNK, A_GROUPS, A_WIDTH // A_GROUPS)
    mixed = jnp.einsum('gpq,bnqgc->bnpgc', w_s, vc) + b_s.T[:, :, None]
    return u * mixed.reshape(bn, tn, A_WIDTH)


def mixer_merge(z, attn, lam_init, ln_g, w_s, b_s, cw, sub_g, wa, wb, wc, wo):
    bn, tn, _ = z.shape
    y_a = spatial_gating(z[..., A_U:A_V], z[..., A_V:B_Q], ln_g, w_s, b_s)
    y_b = (rmsnorm(attn, sub_g) * (1.0 - lam_init)).reshape(bn, tn, B_WIDTH)
    gb, gc, xin = jnp.split(z[..., C_IN:GATE], 3, axis=-1)
    y_c = gb * dwconv3(gc * xin, cw)
    g_a, g_b, g_c = jnp.split(jax.nn.sigmoid(z[..., GATE:]), N_BRANCH, axis=-1)
    m = g_a * (y_a @ wa) + g_b * (y_b @ wb) + g_c * (y_c @ wc)
    return m @ wo


def conv_ffn(h, w_up, cw, cb, w_down):
    u = dwconv3(h @ w_up, cw) + cb
    g, val = jnp.split(u, 2, axis=-1)
    return (jax.nn.silu(g) * val) @ w_down


def setup_inputs(seed: int = 0) -> dict:
    key = jax.random.key(seed)
    ks = jax.random.split(key, 32)
    D = D_MODEL

    def nrm(k, shape, scale):
        return jax.random.normal(k, shape, jnp.float32) * scale

    return {
        'x': nrm(ks[0], (BATCH, SEQ, D), 1.0),
        'c': nrm(ks[1], (BATCH, D), 1.0),
        'ctx': nrm(ks[2], (BATCH, CTX_LEN, D), 1.0),
        'c_ctx': nrm(ks[3], (D,), 1.0),
        'ada_w': nrm(ks[4], (DEPTH, D, 6 * D), 0.5 * D ** -0.5),
        'ada_b': nrm(ks[5], (DEPTH, 6 * D), 0.02),
        'norm1_g': 1.0 + nrm(ks[6], (DEPTH, D), 0.02),
        'norm2_g': 1.0 + nrm(ks[7], (DEPTH, D), 0.02),
        'w_in': nrm(ks[8], (DEPTH, D, IN_COLS), D ** -0.5),
        'sgu_ln_g': 1.0 + nrm(ks[9], (DEPTH, A_WIDTH), 0.02),
        'sgu_w': nrm(ks[10], (DEPTH, A_GROUPS, CHUNK, CHUNK), CHUNK ** -0.5),
        'sgu_b': 1.0 + nrm(ks[11], (DEPTH, A_GROUPS, CHUNK), 0.02),
        'q_norm_g': 1.0 + nrm(ks[12], (DEPTH, B_QK_DIM), 0.02),
        'k_norm_g': 1.0 + nrm(ks[13], (DEPTH, B_QK_DIM), 0.02),
        'lam_q1': nrm(ks[14], (DEPTH, B_QK_DIM), 0.1),
        'lam_k1': nrm(ks[15], (DEPTH, B_QK_DIM), 0.1),
        'lam_q2': nrm(ks[16], (DEPTH, B_QK_DIM), 0.1),
        'lam_k2': nrm(ks[17], (DEPTH, B_QK_DIM), 0.1),
        'subln_g': 1.0 + nrm(ks[18], (DEPTH, B_V_DIM), 0.02),
        'conv_w': nrm(ks[19], (DEPTH, CONV_W, C_WIDTH), CONV_W ** -0.5),
        'w_br_a': nrm(ks[20], (DEPTH, A_WIDTH, D), A_WIDTH ** -0.5),
        'w_br_b': nrm(ks[21], (DEPTH, B_WIDTH, D), B_WIDTH ** -0.5),
        'w_br_c': nrm(ks[22], (DEPTH, C_WIDTH, D), C_WIDTH ** -0.5),
        'w_out': nrm(ks[23], (DEPTH, D, D), D ** -0.5),
        'ffn_up': nrm(ks[24], (DEPTH, D, 2 * D_FF), D ** -0.5),
        'ffn_conv_w': nrm(ks[25], (DEPTH, CONV_W, 2 * D_FF), CONV_W ** -0.5),
        'ffn_conv_b': nrm(ks[26], (DEPTH, 2 * D_FF), 0.02),
        'ffn_down': nrm(ks[27], (DEPTH, D_FF, D), D_FF ** -0.5),
    }


def reference(x, c, ctx, c_ctx, ada_w, ada_b, norm1_g, norm2_g, w_in,
              sgu_ln_g, sgu_w, sgu_b, q_norm_g, k_norm_g,
              lam_q1, lam_k1, lam_q2, lam_k2, subln_g, conv_w,
              w_br_a, w_br_b, w_br_c, w_out,
              ffn_up, ffn_conv_w, ffn_conv_b, ffn_down):
    n_lat = x.shape[1]
    cos, sin = axial_rope_tables(n_lat)
    silu_c = jax.nn.silu(c)
    silu_cc = jax.nn.silu(c_ctx)
    f32 = jnp.float32
    for l in range(DEPTH):
        last = l == DEPTH - 1
        lam_init = 0.8 - 0.6 * math.exp(-0.3 * l)
        lam = (jnp.exp(jnp.sum(lam_q1[l].astype(f32) * lam_k1[l].astype(f32)))
               - jnp.exp(jnp.sum(lam_q2[l].astype(f32) * lam_k2[l].astype(f32))) + lam_init)
        mod_lat = (silu_c @ ada_w[l] + ada_b[l])[:, None, :]
        mod_ctx = silu_cc @ ada_w[l] + ada_b[l]
        sh1, sc1, g1, sh2, sc2, g2 = jnp.split(mod_lat, 6, axis=-1)
        csh1, csc1, cg1, csh2, csc2, cg2 = jnp.split(mod_ctx, 6, axis=-1)

        h_lat = modulate(rmsnorm(x, norm1_g[l]), sh1, sc1)
        h_ctx = modulate(rmsnorm(ctx, norm1_g[l]), csh1, csc1)
        z_lat = h_lat @ w_in[l]
        if last:
            kv_ctx = h_ctx @ w_in[l][:, B_K:C_IN]
        else:
            z_ctx = h_ctx @ w_in[l]
            kv_ctx = z_ctx[..., B_K:C_IN]
        k_ctx = qk_heads(kv_ctx[..., :QK_COLS], k_norm_g[l])
        v_ctx = v_heads(kv_ctx[..., QK_COLS:])
        q_lat = apply_rope(qk_heads(z_lat[..., B_Q:B_K], q_norm_g[l]), cos, sin)
        k_lat = apply_rope(qk_heads(z_lat[..., B_K:B_V], k_norm_g[l]), cos, sin)
        v_lat = v_heads(z_lat[..., B_V:C_IN])
        k_all = jnp.concatenate([k_lat, k_ctx], axis=1)
        v_all = jnp.concatenate([v_lat, v_ctx], axis=1)
        attn_lat = blocked_diff_attention(q_lat, k_all, v_all, lam)
        out_lat = mixer_merge(z_lat, attn_lat, lam_init, sgu_ln_g[l], sgu_w[l], sgu_b[l],
                              conv_w[l], subln_g[l], w_br_a[l], w_br_b[l], w_br_c[l], w_out[l])
        x = x + g1 * out_lat
        h2 = modulate(rmsnorm(x, norm2_g[l]), sh2, sc2)
        x = x + g2 * conv_ffn(h2, ffn_up[l], ffn_conv_w[l], ffn_conv_b[l], ffn_down[l])

        if not last:
            q_ctx = qk_heads(z_ctx[..., B_Q:B_K], q_norm_g[l])
            attn_ctx = diff_attention(q_ctx, k_ctx, v_ctx, lam)
            out_ctx = mixer_merge(z_ctx, attn_ctx, lam_init, sgu_ln_g[l], sgu_w[l], sgu_b[l],
                                  conv_w[l], subln_g[l], w_br_a[l], w_br_b[l], w_br_c[l], w_out[l])
            ctx = ctx + cg1 * out_ctx
            h2c = modulate(rmsnorm(ctx, norm2_g[l]), csh2, csc2)
            ctx = ctx + cg2 * conv_ffn(h2c, ffn_up[l], ffn_conv_w[l], ffn_conv_b[l], ffn_down[l])
    return x
```

```python
import math
from contextlib import ExitStack

import numpy as np
import ml_dtypes

import concourse.bass as bass
import concourse.mybir as mybir
from concourse.bass_utils import run_bass_kernel_spmd

F32 = mybir.dt.float32
BF16 = mybir.dt.bfloat16
AF = mybir.ActivationFunctionType
ALU = mybir.AluOpType
AX = mybir.AxisListType

D = 1024
CTX = 256
GRID_W = 64
A_W = 512
BW = 1024
CW = 512
DFF = 2816
A_U, A_V, B_Q, B_K, B_V, C_IN, GATE, IN_COLS = 0, 512, 1024, 2048, 3072, 4096, 5632, 8704
EPS = 1e-6
ROPE_THETA = 10000.0
GELU_C = 0.7978845608028654

CP_N1, CP_N2, CP_ADAB, CP_CONVW, CP_FCW, CP_FCB, CP_SGUB, NCOLP = 0, 8, 16, 64, 76, 208, 252, 256
RP_LNG, RP_QG, RP_KG, RP_SUBG, RP_LAM, NROWP = 0, 512, 576, 640, 768, 1024


class Eng:
    def __init__(self, name, h, sem):
        self.name, self.h, self.sem = name, h, sem
        self.cnt = 0
        self.seen = {}
        self.mult = 1


class Chan:
    def __init__(self, name, sem):
        self.name, self.sem = name, sem
        self.cnt = 0
        self.mult = 16


class Res:
    __slots__ = ("w", "r")

    def __init__(self):
        self.w = None
        self.r = {}


class Ctx:
    def __init__(self, nc, es):
        self.nc = nc
        self.es = es
        mk = lambda n: es.enter_context(nc.semaphore(n))
        self.pe = Eng("pe", nc.tensor, mk("s_pe"))
        self.act = Eng("act", nc.scalar, mk("s_act"))
        self.dve = Eng("dve", nc.vector, mk("s_dve"))
        self.pool = Eng("pool", nc.gpsimd, mk("s_pool"))
        self.sp = Eng("sp", nc.sync, mk("s_sp"))
        self.engs = [self.pe, self.act, self.dve, self.pool, self.sp]
        self.chans = []
        self.free_chans = {False: [], True: []}
        self.used_chans = []

    def chan(self, name, sw=False):
        if self.free_chans[sw]:
            c = self.free_chans[sw].pop()
        else:
            c = Chan(name, self.es.enter_context(self.nc.semaphore(uid("c_" + name))))
            c.sw = sw
            self.chans.append(c)
        self.used_chans.append(c)
        return c

    def _waits(self, eng, reads, writes, skip_src=None):
        deps = {}
        for r in reads:
            if r.w is not None:
                s, c = r.w
                if deps.get(s, 0) < c:
                    deps[s] = c
        for w in writes:
            if w.w is not None:
                s, c = w.w
                if s is not skip_src and deps.get(s, 0) < c:
                    deps[s] = c
            for s, c in w.r.items():
                if deps.get(s, 0) < c:
                    deps[s] = c
        for s, c in deps.items():
            if s is eng and eng.name == "pe":
                continue
            if eng.seen.get(s, 0) >= c:
                continue
            eng.h.wait_ge(s.sem, c * s.mult)
            eng.seen[s] = c

    def op(self, eng, fn, R=(), W=(), inc=True):
        self._waits(eng, R, W)
        ins = fn()
        if inc:
            eng.cnt += 1
            ins.then_inc(eng.sem, 1)
            c = eng.cnt
        else:
            c = eng.cnt + 1
        for r in R:
            if r.r.get(eng, 0) < c:
                r.r[eng] = c
        for w in W:
            w.w = (eng, c)
            w.r = {}
        return ins

    def dma(self, q, chan, out, in_, R=(), W=()):
        self._waits(q, R, W, skip_src=chan)
        ins = q.h.dma_start(out=out, in_=in_)
        chan.cnt += 1
        ins.then_inc(chan.sem, 16)
        c = chan.cnt
        for r in R:
            if r.r.get(chan, 0) < c:
                r.r[chan] = c
        for w in W:
            w.w = (chan, c)
            w.r = {}
        return ins

    def barrier(self):
        for e in self.engs:
            for s in self.engs + self.chans:
                c = s.cnt
                if c > e.seen.get(s, 0):
                    e.h.wait_ge(s.sem, c * s.mult)
                    e.seen[s] = c
        for c in self.used_chans:
            self.free_chans[c.sw].append(c)
        self.used_chans = []


_UID = [0]


def uid(name):
    _UID[0] += 1
    return f"{name}_{_UID[0]}"


class Slot:
    def __init__(self, t, chan):
        self.t = t
        self.res = Res()
        self.chan = chan


class Ring:
    def __init__(self, cx, es, name, n, shape, dtype, chan=True, sw=False):
        self.slots = []
        for i in range(n):
            t = es.enter_context(cx.nc.sbuf_tensor(uid(f"{name}{i}"), list(shape), dtype))
            self.slots.append(Slot(t, cx.chan(f"{name}{i}", sw=sw) if chan else None))
        self.i = 0

    def next(self):
        s = self.slots[self.i % len(self.slots)]
        self.i += 1
        return s


def build(SEQ=4096, DEPTH=4, debug=False):
    T = SEQ + CTX
    NT = T // 128
    NTL = SEQ // 128
    groups = [(i * 512, 512) for i in range(SEQ // 512)] + [(SEQ, CTX)]
    ranges = [(0, SEQ), (SEQ, T)]
    nc = bass.Bass("TRN2", target_bir_lowering=False)

    def din(name, shape, dt=F32):
        return nc.dram_tensor(name, list(shape), dt, kind="ExternalInput").ap()

    def dscr(name, shape, dt):
        return nc.dram_tensor(name, list(shape), dt, kind="ExternalOutput" if debug else "Internal").ap()

    x_d = din("x", [SEQ, D])
    ctx_d = din("ctx", [CTX, D])
    ccol_d = din("ccol", [128, 16])
    colp_d = din("colp", [DEPTH, 128, NCOLP])
    rowp_d = din("rowp", [DEPTH, 128, NROWP])
    gbias_d = din("gbias", [DEPTH, 128, 2, D])
    sgut_d = din("sgut", [DEPTH, 128, 4, 128])
    cos_d = din("cost", [128, NT, 32])
    sin_d = din("sint", [128, NT, 32])
    identf_d = din("identf", [128, 128])
    identb_d = din("identb", [128, 128], BF16)
    ada_w_d = din("ada_w", [DEPTH, D, 6 * D])
    w_in_d = din("w_in", [DEPTH, D, IN_COLS])
    w_a_d = din("w_br_a", [DEPTH, A_W, D])
    w_b_d = din("w_br_b", [DEPTH, BW, D])
    w_c_d = din("w_br_c", [DEPTH, CW, D])
    w_o_d = din("w_out", [DEPTH, D, D])
    f_up_d = din("ffn_up", [DEPTH, D, 2 * DFF])
    f_dn_d = din("ffn_down", [DEPTH, DFF, D])
    out_d = nc.dram_tensor("out", [SEQ, D], F32, kind="ExternalOutput").ap()

    xres = dscr("xres", [T, D], F32)
    yaT_d = dscr("yaT", [A_W, T], BF16)
    ycT_d = dscr("ycT", [CW, T], BF16)
    gT_d = dscr("gT", [3 * D, T], BF16)
    QT_d = dscr("QT", [8, 128, T], BF16)
    KT_d = dscr("KT", [8, 128, T], BF16)
    V_d = dscr("Vd", [8, 128, NT, 128], BF16)
    ybT_d = dscr("ybT", [BW, T], BF16)
    gated_d = dscr("gated", [DFF, T], BF16)

    es = ExitStack()
    cx = Ctx(nc, es)
    PE, ACT, DVE, POOL, SP = cx.pe, cx.act, cx.dve, cx.pool, cx.sp

    def sb(stack, name, shape, dt):
        return stack.enter_context(nc.sbuf_tensor(uid(name), list(shape), dt))

    def pst(stack, name, shape, dt=F32):
        return stack.enter_context(nc.psum_tensor(uid(name), list(shape), dt))

    H = {}
    identf = sb(es, "identf_s", [128, 128], F32)
    identb = sb(es, "identb_s", [128, 128], BF16)
    ccol = sb(es, "ccol_s", [128, 16], F32)
    csil = sb(es, "csil", [128, 16], F32)
    ccolb = sb(es, "ccolb", [128, 16], BF16)
    crep = sb(es, "crep", [128, 16, 128], BF16)
    colp = sb(es, "colp_s", [128, NCOLP], F32)
    rowp = sb(es, "rowp_s", [128, NROWP], F32)
    modc = sb(es, "modc", [128, 8, 8], F32)
    modraw = sb(es, "modraw", [128, 6, 16], F32)
    gate_rep = sb(es, "gate_rep", [128, 4, D], F32)
    lamt = sb(es, "lamt", [128, 8], F32)
    subg = sb(es, "subg", [128, 128], F32)
    r_const = Res()
    r_layer = Res()
    ch_const = cx.chan("const")

    for dst, src in ((identf, identf_d), (identb, identb_d), (ccol, ccol_d)):
        cx.dma(SP, ch_const, dst[:], src, W=[r_const])
    ch_init = cx.chan("init")
    nchunk = max(1, SEQ // 512)
    for i in range(nchunk):
        a, b = i * (SEQ // nchunk), (i + 1) * (SEQ // nchunk)
        cx.dma(SP, ch_init, xres[a:b, :], x_d[a:b, :])
    cx.dma(SP, ch_init, xres[SEQ:T, :], ctx_d)
    cx.op(ACT, lambda: nc.scalar.activation(out=csil[:], in_=ccol[:], func=AF.Silu), R=[r_const], W=[r_const])
    cx.op(DVE, lambda: nc.vector.tensor_copy(out=ccolb[:], in_=csil[:]), R=[r_const], W=[r_const])
    for j in range(16):
        cx.op(DVE, lambda j=j: nc.vector.tensor_copy(out=crep[:, j, :], in_=csil[:, j:j + 1].to_broadcast([128, 128])),
              R=[r_const], W=[r_const])
    cx.barrier()

    def wsrc(w_ap, c0, n):
        return w_ap.rearrange("(k p) n -> p k n", p=128)[:, :, c0:c0 + n]

    def rstd_act(out_ap, ss_ap, scale, R, W):
        cx.op(ACT, lambda: nc.scalar.activation(out=out_ap, in_=ss_ap, func=AF.Ln, bias=EPS, scale=scale), R=R, W=W)
        cx.op(ACT, lambda: nc.scalar.activation(out=out_ap, in_=out_ap, func=AF.Exp, scale=-0.5), R=W, W=W)

    def phase_mod(l):
        lam_init = 0.8 - 0.6 * math.exp(-0.3 * l)
        with ExitStack() as ps:
            wring = Ring(cx, ps, "wada", 2, [128, 8, 1024], BF16, sw=True)
            pcol = pst(ps, "pcol", [128, 512])
            prow = pst(ps, "prow", [128, 2, 512])
            r_pcol, r_prow = Res(), Res()
            tmp = sb(ps, "lamtmp", [128, 4, 64], F32)
            gbias = sb(ps, "gbias_s", [128, 2, D], F32)
            ch = cx.chan(f"lp{l}")
            cx.dma(SP, ch, colp[:], colp_d[l], W=[r_layer])
            cx.dma(SP, ch, rowp[:], rowp_d[l], W=[r_layer])
            cx.dma(SP, ch, gbias[:], gbias_d[l], W=[r_layer])
            awv = ada_w_d[l]
            for i in range(6):
                ws = wring.next()
                cx.dma(POOL, ws.chan, ws.t[:], wsrc(awv, i * 1024, 1024), W=[ws.res])
                if i in (0, 1, 3, 4):
                    for src in range(2):
                        for j in range(8):
                            for kc in range(8):
                                cx.op(PE, lambda src=src, j=j, kc=kc: nc.tensor.matmul(
                                    pcol[:, src * 8 + j:src * 8 + j + 1], ws.t[:, kc, j * 128:(j + 1) * 128],
                                    ccolb[:, src * 8 + kc:src * 8 + kc + 1], start=(kc == 0), stop=(kc == 7)),
                                    R=[ws.res, r_const], W=[r_pcol], inc=(kc == 7))
                    for src in range(2):
                        cx.op(DVE, lambda src=src: nc.vector.tensor_tensor(
                            out=modraw[:, i, src * 8:src * 8 + 8], in0=pcol[:, src * 8:src * 8 + 8],
                            in1=colp[:, CP_ADAB + i * 8:CP_ADAB + i * 8 + 8], op=ALU.add),
                            R=[r_pcol, r_layer], W=[r_layer])
                else:
                    gi = 0 if i == 2 else 2
                    bidx = 0 if i == 2 else 1
                    for src in range(2):
                        for half in range(2):
                            for kc in range(8):
                                cx.op(PE, lambda src=src, half=half, kc=kc: nc.tensor.matmul(
                                    prow[:, half, :], crep[:, src * 8 + kc, :], ws.t[:, kc, half * 512:(half + 1) * 512],
                                    start=(kc == 0), stop=(kc == 7)),
                                    R=[ws.res, r_const], W=[r_prow], inc=(kc == 7))
                        cx.op(DVE, lambda src=src: nc.vector.tensor_tensor(
                            out=gate_rep[:, gi + src, :], in0=prow[:].rearrange("p a b -> p (a b)"),
                            in1=gbias[:, bidx, :], op=ALU.add),
                            R=[r_prow, r_layer], W=[r_layer])
            for src in range(2):
                for n, (isc, ish, cpn) in enumerate(((1, 0, CP_N1), (4, 3, CP_N2))):
                    kG = src * 4 + n * 2
                    cx.op(DVE, lambda: nc.vector.scalar_tensor_tensor(
                        out=modc[:, kG, :], in0=modraw[:, isc, src * 8:src * 8 + 8], scalar=1.0,
                        in1=colp[:, cpn:cpn + 8], op0=ALU.add, op1=ALU.mult), R=[r_layer], W=[r_layer])
                    cx.op(DVE, lambda: nc.vector.tensor_copy(out=modc[:, kG + 1, :], in_=modraw[:, ish, src * 8:src * 8 + 8]),
                          R=[r_layer], W=[r_layer])
            lv = rowp[:, RP_LAM:RP_LAM + 256].rearrange("p (a d) -> p a d", a=4)
            cx.op(DVE, lambda: nc.vector.tensor_tensor(out=tmp[:, 0:2, :], in0=lv[:, 0:4:2, :], in1=lv[:, 1:4:2, :], op=ALU.mult),
                  R=[r_layer], W=[r_layer])
            cx.op(DVE, lambda: nc.vector.tensor_reduce(out=lamt[:, 0:2], in_=tmp[:, 0:2, :], axis=AX.X, op=ALU.add),
                  R=[r_layer], W=[r_layer])
            cx.op(ACT, lambda: nc.scalar.activation(out=lamt[:, 2:4], in_=lamt[:, 0:2], func=AF.Exp), R=[r_layer], W=[r_layer])
            cx.op(DVE, lambda: nc.vector.tensor_tensor(out=lamt[:, 4:5], in0=lamt[:, 2:3], in1=lamt[:, 3:4], op=ALU.subtract),
                  R=[r_layer], W=[r_layer])
            cx.op(DVE, lambda: nc.vector.tensor_scalar(out=lamt[:, 5:6], in0=lamt[:, 4:5], scalar1=float(lam_init), scalar2=None,
                                                        op0=ALU.add), R=[r_layer], W=[r_layer])
            cx.op(DVE, lambda: nc.vector.tensor_scalar(out=subg[:], in0=rowp[:, RP_SUBG:RP_SUBG + 128],
                                                        scalar1=float(1.0 - lam_init), scalar2=None, op0=ALU.mult),
                  R=[r_layer], W=[r_layer])
            cx.barrier()

    def pipeline(n_items, stages):
        S = len(stages)
        for i in range(n_items + S - 1):
            for si in range(S - 1, -1, -1):
                t = i - si
                if 0 <= t < n_items:
                    stages[si](t)

    def phase_norm(which, tiles):
        with ExitStack() as ps:
            xring = Ring(cx, ps, "nx", 5, [128, D], F32)
            xnring = Ring(cx, ps, "nxn", 3, [128, D], F32, chan=False)
            junk = sb(ps, "njunk", [128, D], BF16)
            r_junk = Res()
            stat = Ring(cx, ps, "nst", 6, [128, 2], F32, chan=False)
            pt = [pst(ps, f"npt{i}", [128, 8, 128]) for i in range(2)]
            r_pt = [Res(), Res()]
            S = {}

            def s0(i):
                t = tiles[i]
                xs = xring.next()
                cx.dma(SP, xs.chan, xs.t[:], xres[t * 128:(t + 1) * 128, :], W=[xs.res])
                S[i] = dict(xs=xs)

            def s1(i):
                d = S[i]
                st = d["st"] = stat.next()
                cx.op(ACT, lambda: nc.scalar.activation(out=junk[:], in_=d["xs"].t[:], func=AF.Square, accum_out=st.t[:, 0:1]),
                      R=[d["xs"].res], W=[r_junk, st.res])

            def s2(i):
                st = S[i]["st"]
                rstd_act(st.t[:, 1:2], st.t[:, 0:1], 1.0 / D, R=[st.res], W=[st.res])

            def s3(i):
                d = S[i]
                xs, st = d["xs"], d["st"]
                xn = xnring.next()
                cx.op(ACT, lambda: nc.scalar.activation(out=xn.t[:], in_=xs.t[:], func=AF.Copy, scale=st.t[:, 1:2]),
                      R=[xs.res, st.res], W=[xn.res])
                p, rp = pt[i % 2], r_pt[i % 2]
                d["p"], d["rp"] = p, rp
                for kc in range(8):
                    cx.op(PE, lambda kc=kc: nc.tensor.transpose(out=p[:, kc, :], in_=xn.t[:, kc * 128:(kc + 1) * 128],
                                                               identity=identf[:]),
                          R=[xn.res], W=[rp], inc=(kc == 7))

            def s4(i):
                d = S.pop(i)
                t = tiles[i]
                p, rp = d["p"], d["rp"]
                kG = (0 if t < NTL else 4) + (0 if which == 1 else 2)
                for kc in range(8):
                    cx.op(DVE, lambda kc=kc: nc.vector.tensor_scalar(
                        out=H["hT"][:, kc, t * 128:(t + 1) * 128], in0=p[:, kc, :], scalar1=modc[:, kG, kc:kc + 1],
                        scalar2=modc[:, kG + 1, kc:kc + 1], op0=ALU.mult, op1=ALU.add), R=[rp], W=[])

            pipeline(len(tiles), [s0, s1, s2, s3, s4])
            cx.barrier()

    def phase_A(l, wv):
        with ExitStack() as ps:
            w = sb(ps, "wA", [128, 8, 1024], BF16)
            sgut = sb(ps, "sgut_s", [128, 4, 128], BF16)
            r_w = Res()
            ch = cx.chan(f"wA{l}", sw=True)
            cx.dma(POOL, ch, w[:], wsrc(wv, A_U, 1024), W=[r_w])
            cx.dma(POOL, ch, sgut[:], sgut_d[l], W=[r_w])
            NP = 3
            puv = [pst(ps, f"puv{i}", [128, 1024]) for i in range(NP)]
            r_puv = [Res() for _ in range(NP)]
            pmix = pst(ps, "pmix", [128, 512])
            r_pmix = Res()
            ptr = pst(ps, "ptrA", [128, 4, 128], BF16)
            r_ptr = Res()
            sq = Ring(cx, ps, "Asq", 3, [128, 1024], F32, chan=False)
            guv = Ring(cx, ps, "Aguv", 5, [128, 1024], F32, chan=False)
            stats = Ring(cx, ps, "Ast", 5, [128, 8], F32, chan=False)
            vn = Ring(cx, ps, "Avn", 3, [128, 512], F32, chan=False)
            vln = Ring(cx, ps, "Avln", 3, [128, 512], BF16, chan=False)
            ya = Ring(cx, ps, "Aya", 3, [128, 512], BF16, chan=False)
            junk = sb(ps, "Ajunk", [128, 512], BF16)
            r_junk = Res()
            rows = sb(ps, "Arows", [128, 4, T], BF16)
            r_rows = Res()
            S = {}

            def s0(t):
                p, rp = puv[t % NP], r_puv[t % NP]
                S[t] = dict(p=p, rp=rp)
                for half in range(2):
                    for kc in range(8):
                        cx.op(PE, lambda half=half, kc=kc: nc.tensor.matmul(
                            p[:, half * 512:(half + 1) * 512], H["hT"][:, kc, t * 128:(t + 1) * 128],
                            w[:, kc, half * 512:(half + 1) * 512], start=(kc == 0), stop=(kc == 7)),
                            R=[r_w], W=[rp], inc=(kc == 7))

            def s1(t):
                d = S[t]
                p, rp = d["p"], d["rp"]
                s1_ = d["sq"] = sq.next()
                cx.op(ACT, lambda: nc.scalar.activation(out=s1_.t[:], in_=p[:], func=AF.Square), R=[rp], W=[s1_.res])
                cx.op(DVE, lambda: nc.vector.tensor_scalar(out=s1_.t[:], in0=s1_.t[:], scalar1=0.044715, scalar2=1.0,
                                                            op0=ALU.mult, op1=ALU.add), R=[s1_.res], W=[s1_.res])
                cx.op(DVE, lambda: nc.vector.tensor_tensor(out=s1_.t[:], in0=s1_.t[:], in1=p[:], op=ALU.mult),
                      R=[s1_.res, rp], W=[s1_.res])

            def s2(t):
                d = S[t]
                p, rp, s1_ = d["p"], d["rp"], d["sq"]
                g1 = d["g"] = guv.next()
                st = d["st"] = stats.next()
                v1 = d["vn"] = vn.next()
                cx.op(ACT, lambda: nc.scalar.activation(out=s1_.t[:], in_=s1_.t[:], func=AF.Sigmoid, scale=2.0 * GELU_C),
                      R=[s1_.res], W=[s1_.res])
                cx.op(DVE, lambda: nc.vector.tensor_tensor(out=g1.t[:], in0=s1_.t[:], in1=p[:], op=ALU.mult),
                      R=[s1_.res, rp], W=[g1.res])
                cx.op(DVE, lambda: nc.vector.tensor_reduce(out=st.t[:, 0:1], in_=g1.t[:, 512:1024], axis=AX.X, op=ALU.add),
                      R=[g1.res], W=[st.res])
                cx.op(DVE, lambda: nc.vector.tensor_scalar(out=st.t[:, 1:2], in0=st.t[:, 0:1], scalar1=1.0 / A_W, scalar2=None,
                                                            op0=ALU.mult), R=[st.res], W=[st.res])
                cx.op(DVE, lambda: nc.vector.tensor_scalar(out=v1.t[:], in0=g1.t[:, 512:1024], scalar1=st.t[:, 1:2], scalar2=None,
                                                            op0=ALU.subtract), R=[g1.res, st.res], W=[v1.res])

            def s3(t):
                d = S[t]
                st, v1 = d["st"], d["vn"]
                vl = d["vl"] = vln.next()
                cx.op(ACT, lambda: nc.scalar.activation(out=junk[:], in_=v1.t[:], func=AF.Square, accum_out=st.t[:, 2:3]),
                      R=[v1.res], W=[r_junk, st.res])
                rstd_act(st.t[:, 3:4], st.t[:, 2:3], 1.0 / A_W, R=[st.res], W=[st.res])
                cx.op(DVE, lambda: nc.vector.scalar_tensor_tensor(
                    out=vl.t[:], in0=v1.t[:], scalar=st.t[:, 3:4], in1=rowp[:, RP_LNG:RP_LNG + 512],
                    op0=ALU.mult, op1=ALU.mult), R=[v1.res, st.res, r_layer], W=[vl.res])

            def s4(t):
                d = S[t]
                vl, g1 = d["vl"], d["g"]
                y1 = d["ya"] = ya.next()
                for g in range(4):
                    cx.op(PE, lambda g=g: nc.tensor.matmul(pmix[:, g * 128:(g + 1) * 128], sgut[:, g, :],
                                                           vl.t[:, g * 128:(g + 1) * 128], start=True, stop=True),
                          R=[vl.res, r_w], W=[r_pmix], inc=(g == 3))
                for g in range(4):
                    cx.op(DVE, lambda g=g: nc.vector.scalar_tensor_tensor(
                        out=y1.t[:, g * 128:(g + 1) * 128], in0=pmix[:, g * 128:(g + 1) * 128],
                        scalar=colp[:, CP_SGUB + g:CP_SGUB + g + 1], in1=g1.t[:, g * 128:(g + 1) * 128],
                        op0=ALU.add, op1=ALU.mult), R=[r_pmix, g1.res, r_layer], W=[y1.res])

            def s5(t):
                y1 = S.pop(t)["ya"]
                for j in range(4):
                    cx.op(PE, lambda j=j: nc.tensor.transpose(out=ptr[:, j, :], in_=y1.t[:, j * 128:(j + 1) * 128],
                                                             identity=identb[:]), R=[y1.res], W=[r_ptr], inc=(j == 3))
                cx.op(ACT, lambda: nc.scalar.copy(out=rows[:, :, t * 128:(t + 1) * 128], in_=ptr[:]), R=[r_ptr], W=[r_rows])

            pipeline(NT, [s0, s1, s2, s3, s4, s5])
            cx.dma(SP, cx.chan("yaTst"), yaT_d.rearrange("(k p) t -> p k t", p=128), rows[:], R=[r_rows])
            cx.barrier()

    def phase_C(l, wv):
        with ExitStack() as ps:
            w = sb(ps, "wC", [128, 8, 3 * CW], BF16)
            r_w = Res()
            ch = cx.chan(f"wC{l}", sw=True)
            cx.dma(POOL, ch, w[:], wsrc(wv, C_IN, 3 * CW), W=[r_w])
            pp = [[pst(ps, f"pC{b}_{i}", [128, 512]) for i in range(3)] for b in range(2)]
            r_pp = [[Res() for _ in range(3)] for _ in range(2)]
            xin_s = Ring(cx, ps, "Cxin", 2, [128, 512], F32, chan=False)
            prow = Ring(cx, ps, "Cp", 2, [128, T], F32, chan=False)
            gbrow = Ring(cx, ps, "Cgb", 2, [128, T], BF16)
            acc = sb(ps, "Cacc", [128, T], F32)
            r_acc = Res()
            it = 0
            for j in range(4):
                pr = prow.next()
                gb = gbrow.next()
                for (g0, n) in groups:
                    P = pp[it % 2]
                    RP = r_pp[it % 2]
                    it += 1
                    for part in range(3):
                        for kc in range(8):
                            cx.op(PE, lambda part=part, kc=kc: nc.tensor.matmul(
                                P[part][:, 0:n], w[:, kc, part * CW + j * 128:part * CW + (j + 1) * 128],
                                H["hT"][:, kc, g0:g0 + n], start=(kc == 0), stop=(kc == 7)),
                                R=[r_w], W=[RP[part]], inc=(kc == 7))
                    xs = xin_s.next()
                    cx.op(ACT, lambda: nc.scalar.copy(out=xs.t[:, 0:n], in_=P[2][:, 0:n]), R=[RP[2]], W=[xs.res])
                    cx.op(DVE, lambda: nc.vector.tensor_tensor(out=pr.t[:, g0:g0 + n], in0=P[1][:, 0:n], in1=xs.t[:, 0:n],
                                                                op=ALU.mult), R=[RP[1], xs.res], W=[pr.res])
                    cx.op(ACT, lambda: nc.scalar.copy(out=gb.t[:, g0:g0 + n], in_=P[0][:, 0:n]), R=[RP[0]], W=[gb.res])
                wc = lambda tap: colp[:, CP_CONVW + tap * 4 + j:CP_CONVW + tap * 4 + j + 1]
                for (a, b) in ranges:
                    cx.op(DVE, lambda: nc.vector.tensor_scalar(out=acc[:, a:b], in0=pr.t[:, a:b], scalar1=wc(1), scalar2=None,
                                                                op0=ALU.mult), R=[pr.res, r_layer], W=[r_acc])
                    cx.op(DVE, lambda: nc.vector.scalar_tensor_tensor(
                        out=acc[:, a + 1:b], in0=pr.t[:, a:b - 1], scalar=wc(0), in1=acc[:, a + 1:b],
                        op0=ALU.mult, op1=ALU.add), R=[pr.res, r_acc], W=[r_acc])
                    cx.op(DVE, lambda: nc.vector.scalar_tensor_tensor(
                        out=acc[:, a:b - 1], in0=pr.t[:, a + 1:b], scalar=wc(2), in1=acc[:, a:b - 1],
                        op0=ALU.mult, op1=ALU.add), R=[pr.res, r_acc], W=[r_acc])
                cx.op(DVE, lambda: nc.vector.tensor_tensor(out=gb.t[:], in0=acc[:], in1=gb.t[:], op=ALU.mult),
                      R=[r_acc, gb.res], W=[gb.res])
                cx.dma(SP, gb.chan, ycT_d[j * 128:(j + 1) * 128, :], gb.t[:], R=[gb.res])
            cx.barrier()

    def phase_G(l, wv):
        with ExitStack() as ps:
            wring = Ring(cx, ps, "wG", 2, [128, 8, 1024], BF16, sw=True)
            pg = [pst(ps, f"pG{i}", [128, 512]) for i in range(4)]
            r_pg = [Res() for _ in range(4)]
            grow = Ring(cx, ps, "Grow", 3, [128, T], BF16)
            it = 0
            for blk in range(3):
                ws = wring.next()
                cx.dma(POOL, ws.chan, ws.t[:], wsrc(wv, GATE + blk * 1024, 1024), W=[ws.res])
                for jj in range(8):
                    gr = grow.next()
                    for (g0, n) in groups:
                        P, RP = pg[it % 4], r_pg[it % 4]
                        it += 1
                        for kc in range(8):
                            cx.op(PE, lambda kc=kc: nc.tensor.matmul(
                                P[:, 0:n], ws.t[:, kc, jj * 128:(jj + 1) * 128], H["hT"][:, kc, g0:g0 + n],
                                start=(kc == 0), stop=(kc == 7)), R=[ws.res], W=[RP], inc=(kc == 7))
                        cx.op(ACT, lambda: nc.scalar.activation(out=gr.t[:, g0:g0 + n], in_=P[:, 0:n], func=AF.Sigmoid),
                              R=[RP], W=[gr.res])
                    j = blk * 8 + jj
                    cx.dma(SP, gr.chan, gT_d[j * 128:(j + 1) * 128, :], gr.t[:], R=[gr.res])
            cx.barrier()

    def phase_V(l, wv):
        with ExitStack() as ps:
            w = sb(ps, "wV", [128, 8, 1024], BF16)
            r_w = Res()
            ch = cx.chan(f"wV{l}", sw=True)
            cx.dma(POOL, ch, w[:], wsrc(wv, B_V, 1024), W=[r_w])
            pv = [pst(ps, f"pV{i}", [128, 1024]) for i in range(2)]
            r_pv = [Res(), Res()]
            stg = Ring(cx, ps, "Vst", 3, [128, 8, 128], BF16)
            vdst = V_d.rearrange("h p t e -> p h t e")
            for t in range(NT):
                P, RP = pv[t % 2], r_pv[t % 2]
                for half in range(2):
                    for kc in range(8):
                        cx.op(PE, lambda half=half, kc=kc: nc.tensor.matmul(
                            P[:, half * 512:(half + 1) * 512], H["hT"][:, kc, t * 128:(t + 1) * 128],
                            w[:, kc, half * 512:(half + 1) * 512], start=(kc == 0), stop=(kc == 7)),
                            R=[r_w], W=[RP], inc=(kc == 7))
                s = stg.next()
                eng = ACT if t % 2 == 0 else DVE
                if eng is ACT:
                    cx.op(ACT, lambda: nc.scalar.copy(out=s.t[:].rearrange("p h e -> p (h e)"), in_=P[:]), R=[RP], W=[s.res])
                else:
                    cx.op(DVE, lambda: nc.vector.tensor_copy(out=s.t[:].rearrange("p h e -> p (h e)"), in_=P[:]), R=[RP], W=[s.res])
                cx.dma(SP, s.chan, vdst[:, :, t, :], s.t[:], R=[s.res])
            cx.barrier()

    def phase_QK(l, wv):
        with ExitStack() as ps:
            w = sb(ps, "wQK", [128, 8, 2048], BF16)
            r_w = Res()
            ch = cx.chan(f"wQKt{l}")
            chw = cx.chan(f"wQK{l}", sw=True)
            r_tab = Res()
            cx.dma(POOL, chw, w[:, :, 0:1024], wsrc(wv, B_Q, 1024), W=[r_w])
            cx.dma(POOL, chw, w[:, :, 1024:2048], wsrc(wv, B_K, 1024), W=[r_w])
            cost = sb(ps, "cost_s", [128, NT, 32], F32)
            sint = sb(ps, "sint_s", [128, NT, 32], F32)
            cx.dma(SP, ch, cost[:], cos_d, W=[r_tab])
            cx.dma(SP, ch, sint[:], sin_d, W=[r_tab])
            NP = 3
            pq = [pst(ps, f"pQK{i}", [128, 1024]) for i in range(NP)]
            r_pq = [Res() for _ in range(NP)]
            ptr = [pst(ps, f"ptrQK{i}", [128, 8, 128], BF16) for i in range(1)]
            r_ptr = [Res()]
            sq = Ring(cx, ps, "QKsq", 2, [128, 1024], F32, chan=False)
            st = Ring(cx, ps, "QKst", 5, [128, 32], F32, chan=False)
            qn = Ring(cx, ps, "QKqn", 4, [128, 1024], F32, chan=False)
            tA = Ring(cx, ps, "QKtA", 2, [128, 512], F32, chan=False)
            tB = Ring(cx, ps, "QKtB", 2, [128, 512], F32, chan=False)
            tC = Ring(cx, ps, "QKtC", 2, [128, 512], F32, chan=False)
            tD = Ring(cx, ps, "QKtD", 2, [128, 512], F32, chan=False)
            qo = Ring(cx, ps, "QKqo", 3, [128, 1024], BF16, chan=False)
            stg = Ring(cx, ps, "QKstg", 3, [128, 8, 128], BF16)
            dsts = (QT_d.rearrange("c p t -> p c t"), KT_d.rearrange("c p t -> p c t"))
            S = {}

            def s0(i):
                t, hf = i // 2, i % 2
                p, rp = pq[i % NP], r_pq[i % NP]
                S[i] = dict(p=p, rp=rp)
                for blk in range(2):
                    for kc in range(8):
                        cx.op(PE, lambda blk=blk, kc=kc: nc.tensor.matmul(
                            p[:, blk * 512:(blk + 1) * 512], H["hT"][:, kc, t * 128:(t + 1) * 128],
                            w[:, kc, hf * 1024 + blk * 512:hf * 1024 + (blk + 1) * 512], start=(kc == 0), stop=(kc == 7)),
                            R=[r_w], W=[rp], inc=(kc == 7))

            def s1(i):
                d = S[i]
                p, rp = d["p"], d["rp"]
                sq1 = sq.next()
                s = d["st"] = st.next()
                cx.op(ACT, lambda: nc.scalar.activation(out=sq1.t[:], in_=p[:], func=AF.Square), R=[rp], W=[sq1.res])
                cx.op(DVE, lambda: nc.vector.tensor_reduce(out=s.t[:, 0:16], in_=sq1.t[:].rearrange("p (h d) -> p h d", d=64),
                                                            axis=AX.X, op=ALU.add), R=[sq1.res], W=[s.res])

            def s2(i):
                d = S[i]
                hf = i % 2
                p, rp, s = d["p"], d["rp"], d["st"]
                q1 = d["qn"] = qn.next()
                rstd_act(s.t[:, 16:32], s.t[:, 0:16], 1.0 / 64, R=[s.res], W=[s.res])
                cx.op(DVE, lambda: nc.vector.tensor_tensor(
                    out=q1.t[:].rearrange("p (h d) -> p h d", d=64), in0=p[:].rearrange("p (h d) -> p h d", d=64),
                    in1=s.t[:, 16:32].unsqueeze(2).to_broadcast([128, 16, 64]), op=ALU.mult),
                    R=[rp, s.res], W=[q1.res])
                gq = rowp[:, RP_QG + hf * 64:RP_QG + (hf + 1) * 64].unsqueeze(1).to_broadcast([128, 16, 64])
                cx.op(POOL, lambda: nc.gpsimd.tensor_tensor(
                    out=q1.t[:].rearrange("p (h d) -> p h d", d=64), in0=q1.t[:].rearrange("p (h d) -> p h d", d=64),
                    in1=gq, op=ALU.mult), R=[q1.res, r_layer], W=[q1.res])

            def s3(i):
                d = S[i]
                t = i // 2
                q1 = d["qn"]
                o1 = d["qo"] = qo.next()
                a1, b1, c1, d1 = tA.next(), tB.next(), tC.next(), tD.next()
                v5 = q1.t[:].rearrange("p (h b s f) -> p h b s f", h=16, b=2, s=2)
                x1, x2 = v5[:, :, :, 0, :], v5[:, :, :, 1, :]
                cs = cost[:, t, :].rearrange("p (b f) -> p b f", b=2).unsqueeze(1).to_broadcast([128, 16, 2, 16])
                sn = sint[:, t, :].rearrange("p (b f) -> p b f", b=2).unsqueeze(1).to_broadcast([128, 16, 2, 16])
                o5 = o1.t[:].rearrange("p (h b s f) -> p h b s f", h=16, b=2, s=2)
                v4 = lambda tt: tt.t[:].rearrange("p (h b f) -> p h b f", h=16, b=2)
                cx.op(DVE, lambda: nc.vector.tensor_tensor(out=v4(a1), in0=x1, in1=cs, op=ALU.mult), R=[q1.res, r_tab], W=[a1.res])
                cx.op(POOL, lambda: nc.gpsimd.tensor_tensor(out=v4(c1), in0=x1, in1=sn, op=ALU.mult), R=[q1.res, r_tab], W=[c1.res])
                cx.op(DVE, lambda: nc.vector.tensor_tensor(out=v4(b1), in0=x2, in1=sn, op=ALU.mult), R=[q1.res, r_tab], W=[b1.res])
                cx.op(POOL, lambda: nc.gpsimd.tensor_tensor(out=v4(d1), in0=x2, in1=cs, op=ALU.mult), R=[q1.res, r_tab], W=[d1.res])
                cx.op(DVE, lambda: nc.vector.tensor_tensor(out=o5[:, :, :, 0, :], in0=v4(a1), in1=v4(b1), op=ALU.subtract),
                      R=[a1.res, b1.res], W=[o1.res])
                cx.op(POOL, lambda: nc.gpsimd.tensor_tensor(out=o5[:, :, :, 1, :], in0=v4(c1), in1=v4(d1), op=ALU.add),
                      R=[c1.res, d1.res], W=[o1.res])

            def s4(i):
                d = S.pop(i)
                t, hf = i // 2, i % 2
                o1 = d["qo"]
                pt_, rpt = ptr[0], r_ptr[0]
                sg = stg.next()
                for c in range(8):
                    cx.op(PE, lambda c=c: nc.tensor.transpose(out=pt_[:, c, :], in_=o1.t[:, c * 128:(c + 1) * 128],
                                                             identity=identb[:]), R=[o1.res], W=[rpt], inc=(c == 7))
                cx.op(ACT, lambda: nc.scalar.copy(out=sg.t[:], in_=pt_[:]), R=[rpt], W=[sg.res])
                cx.dma(SP, sg.chan, dsts[hf][:, :, t * 128:(t + 1) * 128], sg.t[:], R=[sg.res])

            pipeline(2 * NT, [s0, s1, s2, s3, s4])
            cx.barrier()

    def phase_attn(l, do_ctx):
        with ExitStack() as ps:
            ktr = Ring(cx, ps, "aK", 1, [128, 2, T], BF16)
            qpr = Ring(cx, ps, "aQ", 1, [128, 2, 2, T], BF16)
            vring = Ring(cx, ps, "aV", 2, [128, NT, 132], BF16)
            pS = [pst(ps, f"aS{i}", [128, 2, 512]) for i in range(2)]
            r_pS = [Res(), Res()]
            pO = pst(ps, "aO", [128, 3, 512])
            r_pO = Res()
            ptr = pst(ps, "aT", [128, 128], BF16)
            r_ptr = Res()
            pb = Ring(cx, ps, "aP", 3, [128, 2, 512], BF16, chan=False)
            est = Ring(cx, ps, "aE", 10, [128, 16], F32, chan=False)
            t1r = Ring(cx, ps, "aT1", 2, [128, 128], F32, chan=False)
            atr = Ring(cx, ps, "aAt", 10, [128, 128], F32, chan=False)
            ybr = Ring(cx, ps, "aYb", 10, [128, 128], BF16, chan=False)
            junk = sb(ps, "aJunk", [128, 128], BF16)
            r_junk = Res()
            yrow = Ring(cx, ps, "aYrow", 2, [128, T], BF16)
            r_z = Res()
            for s in qpr.slots:
                cx.op(DVE, lambda s=s: nc.vector.memset(s.t[:], 0.0), W=[s.res])
            for s in vring.slots:
                cx.op(POOL, lambda s=s: nc.gpsimd.memset(s.t[:, :, 128:129], 1.0), W=[s.res])

            def oacc(a, qi):
                if qi < 3:
                    return pO[:, a, qi * 129:(qi + 1) * 129]
                return pO[:, 2, a * 129:(a + 1) * 129]

            itc = [0]
            ocopy = Ring(cx, ps, "aOc", 2, [128, 3, 512], F32, chan=False)

            def ocv(oc, a, qi):
                if qi < 3:
                    return oc.t[:, a, qi * 129:(qi + 1) * 129]
                return oc.t[:, 2, a * 129:(a + 1) * 129]

            for hp in range(4):
                ks = ktr.next()
                qs = qpr.next()
                for a in range(2):
                    c = a * 4 + hp
                    cx.dma(SP, ks.chan, ks.t[:, a, :], KT_d[c], W=[ks.res])
                    for hh in range(2):
                        cx.dma(SP, qs.chan, qs.t[hh * 64:(hh + 1) * 64, a, hh, :], QT_d[c, hh * 64:(hh + 1) * 64, :], W=[qs.res])
                for hh in range(2):
                    h = hp * 2 + hh
                    vs = vring.next()
                    cx.dma(SP, vs.chan, vs.t[:, :, 0:128], V_d[h], W=[vs.res])
                    yr = yrow.next()
                    qgroups = groups if do_ctx else groups[:-1]
                    steps = []
                    for (g0, n) in qgroups:
                        kts = list(range(NT)) if g0 < SEQ else list(range(NTL, NT))
                        for ki, kt in enumerate(kts):
                            steps.append((g0, n, ki, kt, ki == len(kts) - 1))

                    def emit_qk(step):
                        g0, n, ki, kt, last = step
                        P, RP = pS[itc[0] % 2], r_pS[itc[0] % 2]
                        itc[0] += 1
                        for a in range(2):
                            cx.op(PE, lambda a=a: nc.tensor.matmul(
                                P[:, a, 0:n], ks.t[:, a, kt * 128:(kt + 1) * 128], qs.t[:, a, hh, g0:g0 + n],
                                start=True, stop=True), R=[ks.res, qs.res], W=[RP], inc=(a == 1))
                        return P, RP

                    deferred = []
                    qk_out = {0: emit_qk(steps[0])}
                    if len(steps) > 1:
                        qk_out[1] = emit_qk(steps[1])
                    first_in_bank = {0: True, 1: True, 2: True}
                    for si, step in enumerate(steps):
                        g0, n, ki, kt, last = step
                        nq = n // 128
                        P, RP = qk_out.pop(si)
                        if ki == 0:
                            first_in_bank = {0: True, 1: True, 2: True}
                        pbs = pb.next()
                        cx.op(ACT, lambda: nc.scalar.activation(out=pbs.t[:, :, 0:n], in_=P[:, :, 0:n], func=AF.Exp, scale=0.125),
                              R=[RP], W=[pbs.res])
                        if si + 2 < len(steps):
                            qk_out[si + 2] = emit_qk(steps[si + 2])
                        for a in range(2):
                            for qi in range(nq):
                                bank = a if qi < 3 else 2
                                st_flag = first_in_bank[bank] and ki == 0
                                first_in_bank[bank] = False
                                cx.op(PE, lambda a=a, qi=qi, st_flag=st_flag: nc.tensor.matmul(
                                    oacc(a, qi), pbs.t[:, a, qi * 128:(qi + 1) * 128], vs.t[:, kt, 0:129],
                                    start=st_flag, stop=last, skip_group_check=True),
                                    R=[pbs.res, vs.res], W=[r_pO], inc=(a == 1 and qi == nq - 1))
                        if deferred and not last:
                            deferred.pop(0)()
                        if not last:
                            continue
                        oc = ocopy.next()
                        for b in range(3):
                            cx.op(DVE, lambda b=b: nc.vector.tensor_copy(out=oc.t[:, b, :], in_=pO[:, b, :]), R=[r_pO], W=[oc.res])
                        stA, stB, stC = [], [], []
                        for qi in range(nq):
                            e = est.next()
                            at, yb = atr.next(), ybr.next()
                            o0, o1 = ocv(oc, 0, qi), ocv(oc, 1, qi)
                            q0 = g0 + qi * 128

                            def fA(e=e, at=at, o0=o0, o1=o1, oc=oc):
                                t1 = t1r.next()
                                cx.op(DVE, lambda: nc.vector.reciprocal(out=e.t[:, 0:1], in_=o0[:, 128:129]), R=[oc.res], W=[e.res])
                                cx.op(DVE, lambda: nc.vector.reciprocal(out=e.t[:, 1:2], in_=o1[:, 128:129]), R=[oc.res], W=[e.res])
                                cx.op(DVE, lambda: nc.vector.tensor_tensor(out=e.t[:, 2:3], in0=e.t[:, 1:2], in1=lamt[:, 5:6], op=ALU.mult),
                                      R=[e.res, r_layer], W=[e.res])
                                cx.op(DVE, lambda: nc.vector.tensor_scalar(out=t1.t[:], in0=o1[:, 0:128], scalar1=e.t[:, 2:3], scalar2=None,
                                                                            op0=ALU.mult), R=[oc.res, e.res], W=[t1.res])
                                cx.op(DVE, lambda: nc.vector.scalar_tensor_tensor(
                                    out=at.t[:], in0=o0[:, 0:128], scalar=e.t[:, 0:1], in1=t1.t[:], op0=ALU.mult, op1=ALU.subtract),
                                    R=[oc.res, e.res, t1.res], W=[at.res])

                            def fB(e=e, at=at, yb=yb):
                                cx.op(ACT, lambda: nc.scalar.activation(out=junk[:], in_=at.t[:], func=AF.Square, accum_out=e.t[:, 3:4]),
                                      R=[at.res], W=[r_junk, e.res])
                                rstd_act(e.t[:, 4:5], e.t[:, 3:4], 1.0 / 128, R=[e.res], W=[e.res])
                                cx.op(DVE, lambda: nc.vector.scalar_tensor_tensor(
                                    out=yb.t[:], in0=at.t[:], scalar=e.t[:, 4:5], in1=subg[:], op0=ALU.mult, op1=ALU.mult),
                                    R=[at.res, e.res, r_layer], W=[yb.res])

                            def fC(yb=yb, q0=q0):
                                cx.op(PE, lambda: nc.tensor.transpose(out=ptr[:], in_=yb.t[:], identity=identb[:]), R=[yb.res], W=[r_ptr])
                                cx.op(DVE, lambda: nc.vector.tensor_copy(out=yr.t[:, q0:q0 + 128], in_=ptr[:]), R=[r_ptr], W=[yr.res])
                            stA.append(fA)
                            stB.append(fB)
                            stC.append(fC)
                        deferred.extend(stA + stB + stC)
                    for fn in deferred:
                        fn()
                    deferred = []
                    hi = T if do_ctx else SEQ
                    cx.dma(SP, yr.chan, ybT_d[h * 128:(h + 1) * 128, 0:hi], yr.t[:, 0:hi], R=[yr.res])
            cx.barrier()

    def resid_update(P, RP, xs, xo, gidx, dst_ap):
        cx.op(DVE, lambda: nc.vector.tensor_tensor(out=xo.t[:], in0=P[:], in1=gate_rep[:, gidx, :], op=ALU.mult),
              R=[RP, r_layer], W=[xo.res])
        cx.op(POOL, lambda: nc.gpsimd.tensor_tensor(out=xo.t[:], in0=xo.t[:], in1=xs.t[:], op=ALU.add),
              R=[xo.res, xs.res], W=[xo.res])
        cx.dma(SP, xo.chan, dst_ap, xo.t[:], R=[xo.res])

    def merge_weights(l, stack):
        wa = sb(stack, "wa", [128, 4, D], BF16)
        wb = sb(stack, "wb", [128, 8, D], BF16)
        wc = sb(stack, "wc", [128, 4, D], BF16)
        wo = sb(stack, "wo", [128, 8, D], BF16)
        rs = [Res() for _ in range(4)]
        for wt, src, r in ((wa, w_a_d, rs[0]), (wb, w_b_d, rs[1]), (wc, w_c_d, rs[2]), (wo, w_o_d, rs[3])):
            cx.dma(POOL, cx.chan(f"wM{l}", sw=True), wt[:], wsrc(src[l], 0, D), W=[r])
        return (wa, wb, wc, wo), rs

    def phase_merge(l, do_ctx, mw):
        with ExitStack() as ps:
            (wa, wb, wc, wo), (r_wa, r_wb, r_wc, r_wo) = mw
            inr = Ring(cx, ps, "mIn", 2, [128, 40, 512], BF16)
            pbr = [[pst(ps, f"mP{b}_{i}", [128, 512]) for i in range(3)] for b in range(2)]
            r_pbr = [[Res() for _ in range(3)] for _ in range(2)]
            po = [pst(ps, f"mO{i}", [128, 1024]) for i in range(1)]
            r_po = [Res()]
            ta = Ring(cx, ps, "mTa", 2, [128, 512], F32, chan=False)
            tb = Ring(cx, ps, "mTb", 2, [128, 512], F32, chan=False)
            tc = Ring(cx, ps, "mTc", 2, [128, 512], F32, chan=False)
            mT = Ring(cx, ps, "mMT", 2, [128, 8, 512], BF16, chan=False)
            xin = Ring(cx, ps, "mX", 3, [128, D], F32)
            xout = Ring(cx, ps, "mXo", 2, [128, D], F32)
            gl = groups if do_ctx else groups[:-1]
            ins = {}

            def load(gi):
                g0, n = gl[gi]
                s = inr.next()
                cx.dma(SP, s.chan, s.t[:, 0:4, 0:n], yaT_d.rearrange("(k p) t -> p k t", p=128)[:, :, g0:g0 + n], W=[s.res])
                cx.dma(SP, s.chan, s.t[:, 4:12, 0:n], ybT_d.rearrange("(k p) t -> p k t", p=128)[:, :, g0:g0 + n], W=[s.res])
                cx.dma(SP, s.chan, s.t[:, 12:16, 0:n], ycT_d.rearrange("(k p) t -> p k t", p=128)[:, :, g0:g0 + n], W=[s.res])
                cx.dma(SP, s.chan, s.t[:, 16:40, 0:n], gT_d.rearrange("(k p) t -> p k t", p=128)[:, :, g0:g0 + n], W=[s.res])
                ins[gi] = s

            load(0)
            ot = 0
            for gi, (g0, n) in enumerate(gl):
                if gi + 1 < len(gl):
                    load(gi + 1)
                s = ins.pop(gi)
                m = mT.next()
                xl = []
                for qi in range(n // 128):
                    xs = xin.next() if qi < 3 else None
                    if xs is not None:
                        tix = g0 // 128 + qi
                        cx.dma(SP, xs.chan, xs.t[:], xres[tix * 128:(tix + 1) * 128, :], W=[xs.res])
                    xl.append(xs)
                for oc in range(8):
                    P, RP = pbr[oc % 2], r_pbr[oc % 2]
                    for br, (wt, nk, off, rwt) in enumerate(((wa, 4, 0, r_wa), (wb, 8, 4, r_wb), (wc, 4, 12, r_wc))):
                        for kc in range(nk):
                            cx.op(PE, lambda br=br, wt=wt, kc=kc, off=off, nk=nk: nc.tensor.matmul(
                                P[br][:, 0:n], wt[:, kc, oc * 128:(oc + 1) * 128], s.t[:, off + kc, 0:n],
                                start=(kc == 0), stop=(kc == nk - 1)), R=[rwt, s.res], W=[RP[br]], inc=(kc == nk - 1))
                    a1, b1, c1 = ta.next(), tb.next(), tc.next()
                    cx.op(DVE, lambda: nc.vector.tensor_tensor(out=a1.t[:, 0:n], in0=P[0][:, 0:n], in1=s.t[:, 16 + oc, 0:n], op=ALU.mult),
                          R=[RP[0], s.res], W=[a1.res])
                    cx.op(DVE, lambda: nc.vector.tensor_tensor(out=b1.t[:, 0:n], in0=P[1][:, 0:n], in1=s.t[:, 24 + oc, 0:n], op=ALU.mult),
                          R=[RP[1], s.res], W=[b1.res])
                    cx.op(DVE, lambda: nc.vector.tensor_tensor(out=c1.t[:, 0:n], in0=P[2][:, 0:n], in1=s.t[:, 32 + oc, 0:n], op=ALU.mult),
                          R=[RP[2], s.res], W=[c1.res])
                    cx.op(POOL, lambda: nc.gpsimd.tensor_tensor(out=a1.t[:, 0:n], in0=a1.t[:, 0:n], in1=b1.t[:, 0:n], op=ALU.add),
                          R=[a1.res, b1.res], W=[a1.res])
                    cx.op(POOL, lambda: nc.gpsimd.tensor_tensor(out=m.t[:, oc, 0:n], in0=a1.t[:, 0:n], in1=c1.t[:, 0:n], op=ALU.add),
                          R=[a1.res, c1.res], W=[m.res])
                for qi in range(n // 128):
                    tix = g0 // 128 + qi
                    xs = xl[qi]
                    if xs is None:
                        xs = xin.next()
                        cx.dma(SP, xs.chan, xs.t[:], xres[tix * 128:(tix + 1) * 128, :], W=[xs.res])
                    P, RP = po[0], r_po[0]
                    ot += 1
                    for half in range(2):
                        for kc in range(8):
                            cx.op(PE, lambda half=half, kc=kc: nc.tensor.matmul(
                                P[:, half * 512:(half + 1) * 512], m.t[:, kc, qi * 128:(qi + 1) * 128],
                                wo[:, kc, half * 512:(half + 1) * 512], start=(kc == 0), stop=(kc == 7)),
                                R=[m.res, r_wo], W=[RP], inc=(kc == 7))
                    xo = xout.next()
                    resid_update(P, RP, xs, xo, 0 if tix < NTL else 1, xres[tix * 128:(tix + 1) * 128, :])
            cx.barrier()

    def phase_ffn_up(l, do_ctx):
        with ExitStack() as ps:
            wring = Ring(cx, ps, "wF", 2, [128, 8, 1024], BF16, sw=True)
            pf = [[pst(ps, f"pF{b}_{i}", [128, 512]) for i in range(2)] for b in range(2)]
            r_pf = [[Res(), Res()] for _ in range(2)]
            raw = Ring(cx, ps, "Fraw", 2, [128, 2, T], BF16)
            accg = sb(ps, "Faccg", [128, T], F32)
            accv = sb(ps, "Faccv", [128, T], F32)
            r_ag, r_av = Res(), Res()
            fup = f_up_d[l]
            gl = groups if do_ctx else groups[:-1]
            rl = ranges if do_ctx else ranges[:1]
            hi = T if do_ctx else SEQ
            it = 0
            pend = []
            NCH = DFF // 128
            for blk in range((NCH + 3) // 4):
                j0 = blk * 4
                nj = min(4, NCH - j0)
                ws = wring.next()
                cx.dma(POOL, ws.chan, ws.t[:, :, 0:nj * 128], wsrc(fup, j0 * 128, nj * 128), W=[ws.res])
                cx.dma(POOL, ws.chan, ws.t[:, :, 512:512 + nj * 128], wsrc(fup, DFF + j0 * 128, nj * 128), W=[ws.res])
                for jj in range(nj):
                    j = j0 + jj
                    rw = raw.next()
                    for gidx, (g0, n) in enumerate(gl):
                        if gidx == len(gl) - 1:
                            for fn in pend:
                                fn()
                            pend = []
                        P, RP = pf[it % 2], r_pf[it % 2]
                        it += 1
                        for part in range(2):
                            for kc in range(8):
                                cx.op(PE, lambda part=part, kc=kc: nc.tensor.matmul(
                                    P[part][:, 0:n], ws.t[:, kc, part * 512 + jj * 128:part * 512 + (jj + 1) * 128],
                                    H["hT"][:, kc, g0:g0 + n], start=(kc == 0), stop=(kc == 7)),
                                    R=[ws.res], W=[RP[part]], inc=(kc == 7))
                        cx.op(ACT, lambda: nc.scalar.copy(out=rw.t[:, 0, g0:g0 + n], in_=P[0][:, 0:n]), R=[RP[0]], W=[rw.res])
                        cx.op(ACT, lambda: nc.scalar.copy(out=rw.t[:, 1, g0:g0 + n], in_=P[1][:, 0:n]), R=[RP[1]], W=[rw.res])
                    for part, (acc, racc, eng, eh) in enumerate(((accg, r_ag, DVE, nc.vector), (accv, r_av, DVE, nc.vector))):
                        cj = part * NCH + j
                        wcol = lambda tap, cj=cj: colp[:, CP_FCW + tap * 44 + cj:CP_FCW + tap * 44 + cj + 1]
                        bcol = colp[:, CP_FCB + cj:CP_FCB + cj + 1]
                        for (a, b) in rl:
                            cx.op(eng, lambda: eh.tensor_scalar(
                                out=acc[:, a:b], in0=rw.t[:, part, a:b], scalar1=wcol(1), scalar2=bcol,
                                op0=ALU.mult, op1=ALU.add), R=[rw.res, r_layer], W=[racc])
                            cx.op(eng, lambda: eh.scalar_tensor_tensor(
                                out=acc[:, a + 1:b], in0=rw.t[:, part, a:b - 1], scalar=wcol(0), in1=acc[:, a + 1:b],
                                op0=ALU.mult, op1=ALU.add), R=[rw.res, racc], W=[racc])
                            cx.op(eng, lambda: eh.scalar_tensor_tensor(
                                out=acc[:, a:b - 1], in0=rw.t[:, part, a + 1:b], scalar=wcol(2), in1=acc[:, a:b - 1],
                                op0=ALU.mult, op1=ALU.add), R=[rw.res, racc], W=[racc])

                    def tail(rw=rw, j=j):
                        cx.op(ACT, lambda: nc.scalar.activation(out=accg[:, 0:hi], in_=accg[:, 0:hi], func=AF.Silu), R=[r_ag], W=[r_ag])
                        cx.op(POOL, lambda: nc.gpsimd.tensor_tensor(out=rw.t[:, 0, 0:hi], in0=accg[:, 0:hi], in1=accv[:, 0:hi], op=ALU.mult),
                              R=[r_ag, r_av, rw.res], W=[rw.res])
                        cx.dma(SP, rw.chan, gated_d[j * 128:(j + 1) * 128, 0:hi], rw.t[:, 0, 0:hi], R=[rw.res])
                    pend.append(tail)
            for fn in pend:
                fn()
            cx.barrier()

    def phase_ffn_down(l, do_ctx, final):
        with ExitStack() as ps:
            NCH = DFF // 128
            wd = sb(ps, "wd", [128, NCH, D], BF16)
            r_wd = [Res(), Res()]
            for i in range(2):
                cx.dma(POOL, cx.chan(f"wD{l}", sw=True), wd[:, i * 11:(i + 1) * 11, :],
                       f_dn_d[l].rearrange("(k p) n -> p k n", p=128)[:, i * 11:(i + 1) * 11, :], W=[r_wd[i]])
            inr = Ring(cx, ps, "dIn", 2, [128, NCH, 512], BF16)
            po = [pst(ps, f"dO{i}", [128, 1024]) for i in range(2)]
            r_po = [Res(), Res()]
            xin = Ring(cx, ps, "dX", 3, [128, D], F32)
            xout = Ring(cx, ps, "dXo", 2, [128, D], F32)
            gl = groups if do_ctx else groups[:-1]
            ins = {}

            def load(gi):
                g0, n = gl[gi]
                s = inr.next()
                cx.dma(SP, s.chan, s.t[:, :, 0:n], gated_d.rearrange("(k p) t -> p k t", p=128)[:, :, g0:g0 + n], W=[s.res])
                ins[gi] = s

            load(0)
            ot = 0
            for gi, (g0, n) in enumerate(gl):
                if gi + 1 < len(gl):
                    load(gi + 1)
                s = ins.pop(gi)
                for qi in range(n // 128):
                    tix = g0 // 128 + qi
                    xs = xin.next()
                    cx.dma(SP, xs.chan, xs.t[:], xres[tix * 128:(tix + 1) * 128, :], W=[xs.res])
                    P, RP = po[ot % 2], r_po[ot % 2]
                    ot += 1
                    for half in range(2):
                        for kc in range(NCH):
                            cx.op(PE, lambda half=half, kc=kc: nc.tensor.matmul(
                                P[:, half * 512:(half + 1) * 512], s.t[:, kc, qi * 128:(qi + 1) * 128],
                                wd[:, kc, half * 512:(half + 1) * 512], start=(kc == 0), stop=(kc == NCH - 1)),
                                R=[s.res, r_wd[kc // 11]], W=[RP], inc=(kc == NCH - 1))
                    xo = xout.next()
                    if final:
                        dst = out_d[tix * 128:(tix + 1) * 128, :]
                    else:
                        dst = xres[tix * 128:(tix + 1) * 128, :]
                    resid_update(P, RP, xs, xo, 2 if tix < NTL else 3, dst)
            cx.barrier()

    for l in range(DEPTH):
        last = (l == DEPTH - 1)
        do_ctx = not last
        wv = w_in_d[l]
        phase_mod(l)
        with ExitStack() as hs:
            H["hT"] = sb(hs, f"hT1_{l}", [128, 8, T], BF16)
            phase_norm(1, list(range(NT)))
            phase_A(l, wv)
            phase_C(l, wv)
            phase_G(l, wv)
            phase_V(l, wv)
            phase_QK(l, wv)
        with ExitStack() as ms:
            mw = merge_weights(l, ms)
            phase_attn(l, do_ctx)
            phase_merge(l, do_ctx, mw)
        with ExitStack() as hs:
            H["hT"] = sb(hs, f"hT2_{l}", [128, 8, T], BF16)
            phase_norm(2, list(range(NT if do_ctx else NTL)))
            phase_ffn_up(l, do_ctx)
        phase_ffn_down(l, do_ctx, last)
    cx.barrier()
    es.close()
    return nc


def _rope_tables(SEQ):
    T = SEQ + CTX
    rows = SEQ // GRID_W
    row = np.repeat(np.arange(rows, dtype=np.float32), GRID_W)
    col = np.tile(np.arange(GRID_W, dtype=np.float32), rows)
    n_freq = 16
    inv = (np.float32(ROPE_THETA) ** (-np.arange(n_freq, dtype=np.float32) / np.float32(n_freq))).astype(np.float32)
    ang = np.stack([row[:, None] * inv, col[:, None] * inv], axis=1).astype(np.float32)
    cos = np.ones((T, 2, n_freq), np.float32)
    sin = np.zeros((T, 2, n_freq), np.float32)
    cos[:SEQ] = np.cos(ang)
    sin[:SEQ] = np.sin(ang)
    NT = T // 128
    cos = cos.reshape(NT, 128, 32).transpose(1, 0, 2)
    sin = sin.reshape(NT, 128, 32).transpose(1, 0, 2)
    return np.ascontiguousarray(cos), np.ascontiguousarray(sin)


def _col(v, nchunk):
    return np.ascontiguousarray(np.asarray(v, np.float32).reshape(nchunk, 128).T)


def prep_shared(inp, SEQ, DEPTH):
    f = lambda k: np.asarray(inp[k], np.float32)
    colp = np.zeros((DEPTH, 128, NCOLP), np.float32)
    rowp = np.zeros((DEPTH, 128, NROWP), np.float32)
    gbias = np.zeros((DEPTH, 128, 2, D), np.float32)
    for l in range(DEPTH):
        colp[l, :, CP_N1:CP_N1 + 8] = _col(f("norm1_g")[l], 8)
        colp[l, :, CP_N2:CP_N2 + 8] = _col(f("norm2_g")[l], 8)
        colp[l, :, CP_ADAB:CP_ADAB + 48] = _col(f("ada_b")[l], 48)
        cw = f("conv_w")[l]
        for tap in range(3):
            colp[l, :, CP_CONVW + tap * 4:CP_CONVW + tap * 4 + 4] = _col(cw[tap], 4)
        fw = f("ffn_conv_w")[l]
        for tap in range(3):
            colp[l, :, CP_FCW + tap * 44:CP_FCW + tap * 44 + 44] = _col(fw[tap], 44)
        colp[l, :, CP_FCB:CP_FCB + 44] = _col(f("ffn_conv_b")[l], 44)
        colp[l, :, CP_SGUB:CP_SGUB + 4] = f("sgu_b")[l].T
        row = np.concatenate([
            f("sgu_ln_g")[l], f("q_norm_g")[l], f("k_norm_g")[l], f("subln_g")[l],
            f("lam_q1")[l], f("lam_k1")[l], f("lam_q2")[l], f("lam_k2")[l]])
        rowp[l] = np.broadcast_to(row[None, :], (128, NROWP))
        gbias[l, :, 0, :] = f("ada_b")[l][None, 2 * D:3 * D]
        gbias[l, :, 1, :] = f("ada_b")[l][None, 5 * D:6 * D]
    sgut = np.ascontiguousarray(f("sgu_w").transpose(0, 3, 1, 2))
    cos, sin = _rope_tables(SEQ)
    shared = {
        "colp": colp, "rowp": rowp, "gbias": gbias, "sgut": sgut, "cost": cos, "sint": sin,
        "identf": np.eye(128, dtype=np.float32), "identb": np.eye(128, dtype=np.float32).astype(ml_dtypes.bfloat16),
    }
    for k in ("ada_w", "w_in", "w_br_a", "w_br_b", "w_br_c", "w_out", "ffn_up", "ffn_down"):
        shared[k] = np.ascontiguousarray(f(k))
    return shared


def core_inputs(inp, shared, b):
    m = dict(shared)
    m["x"] = np.ascontiguousarray(np.asarray(inp["x"][b], np.float32))
    m["ctx"] = np.ascontiguousarray(np.asarray(inp["ctx"][b], np.float32))
    m["ccol"] = np.ascontiguousarray(np.concatenate([_col(inp["c"][b], 8), _col(inp["c_ctx"], 8)], axis=1))
    return m


def kernel(**inputs):
    B, SEQ, _ = inputs["x"].shape
    DEPTH = inputs["w_in"].shape[0]
    nc = build(SEQ=SEQ, DEPTH=DEPTH)
    shared = prep_shared(inputs, SEQ, DEPTH)
    in_maps = [core_inputs(inputs, shared, b) for b in range(B)]
    res = run_bass_kernel_spmd(nc, in_maps, core_ids=list(range(B)))
    return np.stack([np.asarray(r["out"], np.float32) for r in res.results], axis=0)
```

```python
import math
from contextlib import ExitStack

import numpy as np
import ml_dtypes

import concourse.bass as bass
import concourse.mybir as mybir
from concourse.bass_utils import run_bass_kernel_spmd

F32 = mybir.dt.float32
BF16 = mybir.dt.bfloat16
AF = mybir.ActivationFunctionType
ALU = mybir.AluOpType
AX = mybir.AxisListType

D = 1024
CTX = 256
GRID_W = 64
A_W = 512
BW = 1024
CW = 512
DFF = 2816
A_U, A_V, B_Q, B_K, B_V, C_IN, GATE, IN_COLS = 0, 512, 1024, 2048, 3072, 4096, 5632, 8704
EPS = 1e-6
ROPE_THETA = 10000.0
GELU_C = 0.7978845608028654

CP_N1, CP_N2, CP_ADAB, CP_CONVW, CP_FCW, CP_FCB, CP_SGUB, NCOLP = 0, 8, 16, 64, 76, 208, 252, 256
RP_LNG, RP_QG, RP_KG, RP_SUBG, RP_LAM, NROWP = 0, 512, 576, 640, 768, 1024


class Eng:
    def __init__(self, name, h, sem):
        self.name, self.h, self.sem = name, h, sem
        self.cnt = 0
        self.seen = {}
        self.mult = 1


class Chan:
    def __init__(self, name, sem):
        self.name, self.sem = name, sem
        self.cnt = 0
        self.mult = 16


class Res:
    __slots__ = ("w", "r")

    def __init__(self):
        self.w = None
        self.r = {}


class Ctx:
    def __init__(self, nc, es):
        self.nc = nc
        self.es = es
        mk = lambda n: es.enter_context(nc.semaphore(n))
        self.pe = Eng("pe", nc.tensor, mk("s_pe"))
        self.act = Eng("act", nc.scalar, mk("s_act"))
        self.dve = Eng("dve", nc.vector, mk("s_dve"))
        self.pool = Eng("pool", nc.gpsimd, mk("s_pool"))
        self.sp = Eng("sp", nc.sync, mk("s_sp"))
        self.engs = [self.pe, self.act, self.dve, self.pool, self.sp]
        self.chans = []
        self.free_chans = {False: [], True: []}
        self.used_chans = []

    def chan(self, name, sw=False):
        if self.free_chans[sw]:
            c = self.free_chans[sw].pop()
        else:
            c = Chan(name, self.es.enter_context(self.nc.semaphore(uid("c_" + name))))
            c.sw = sw
            self.chans.append(c)
        self.used_chans.append(c)
        return c

    def _waits(self, eng, reads, writes, skip_src=None):
        deps = {}
        for r in reads:
            if r.w is not None:
                s, c = r.w
                if deps.get(s, 0) < c:
                    deps[s] = c
        for w in writes:
            if w.w is not None:
                s, c = w.w
                if s is not skip_src and deps.get(s, 0) < c:
                    deps[s] = c
            for s, c in w.r.items():
                if deps.get(s, 0) < c:
                    deps[s] = c
        for s, c in deps.items():
            if s is eng and eng.name == "pe":
                continue
            if eng.seen.get(s, 0) >= c:
                continue
            eng.h.wait_ge(s.sem, c * s.mult)
            eng.seen[s] = c

    def op(self, eng, fn, R=(), W=(), inc=True):
        self._waits(eng, R, W)
        ins = fn()
        if inc:
            eng.cnt += 1
            ins.then_inc(eng.sem, 1)
            c = eng.cnt
        else:
            c = eng.cnt + 1
        for r in R:
            if r.r.get(eng, 0) < c:
                r.r[eng] = c
        for w in W:
            w.w = (eng, c)
            w.r = {}
        return ins

    def dma(self, q, chan, out, in_, R=(), W=()):
        self._waits(q, R, W, skip_src=chan)
        ins = q.h.dma_start(out=out, in_=in_)
        chan.cnt += 1
        ins.then_inc(chan.sem, 16)
        c = chan.cnt
        for r in R:
            if r.r.get(chan, 0) < c:
                r.r[chan] = c
        for w in W:
            w.w = (chan, c)
            w.r = {}
        return ins

    def barrier(self):
        for e in self.engs:
            for s in self.engs + self.chans:
                c = s.cnt
                if c > e.seen.get(s, 0):
                    e.h.wait_ge(s.sem, c * s.mult)
                    e.seen[s] = c
        for c in self.used_chans:
            self.free_chans[c.sw].append(c)
        self.used_chans = []


_UID = [0]


def uid(name):
    _UID[0] += 1
    return f"{name}_{_UID[0]}"


class Slot:
    def __init__(self, t, chan):
        self.t = t
        self.res = Res()
        self.chan = chan


class Ring:
    def __init__(self, cx, es, name, n, shape, dtype, chan=True, sw=False):
        self.slots = []
        for i in range(n):
            t = es.enter_context(cx.nc.sbuf_tensor(uid(f"{name}{i}"), list(shape), dtype))
            self.slots.append(Slot(t, cx.chan(f"{name}{i}", sw=sw) if chan else None))
        self.i = 0

    def next(self):
        s = self.slots[self.i % len(self.slots)]
        self.i += 1
        return s


def build(SEQ=4096, DEPTH=4, debug=False):
    T = SEQ + CTX
    NT = T // 128
    NTL = SEQ // 128
    groups = [(i * 512, 512) for i in range(SEQ // 512)] + [(SEQ, CTX)]
    ranges = [(0, SEQ), (SEQ, T)]
    nc = bass.Bass("TRN2", target_bir_lowering=False)

    def din(name, shape, dt=F32):
        return nc.dram_tensor(name, list(shape), dt, kind="ExternalInput").ap()

    def dscr(name, shape, dt):
        return nc.dram_tensor(name, list(shape), dt, kind="ExternalOutput" if debug else "Internal").ap()

    x_d = din("x", [SEQ, D])
    ctx_d = din("ctx", [CTX, D])
    ccol_d = din("ccol", [128, 16])
    colp_d = din("colp", [DEPTH, 128, NCOLP])
    rowp_d = din("rowp", [DEPTH, 128, NROWP])
    gbias_d = din("gbias", [DEPTH, 128, 2, D])
    sgut_d = din("sgut", [DEPTH, 128, 4, 128])
    cos_d = din("cost", [128, NT, 32])
    sin_d = din("sint", [128, NT, 32])
    identf_d = din("identf", [128, 128])
    identb_d = din("identb", [128, 128], BF16)
    ada_w_d = din("ada_w", [DEPTH, D, 6 * D])
    w_in_d = din("w_in", [DEPTH, D, IN_COLS])
    w_a_d = din("w_br_a", [DEPTH, A_W, D])
    w_b_d = din("w_br_b", [DEPTH, BW, D])
    w_c_d = din("w_br_c", [DEPTH, CW, D])
    w_o_d = din("w_out", [DEPTH, D, D])
    f_up_d = din("ffn_up", [DEPTH, D, 2 * DFF])
    f_dn_d = din("ffn_down", [DEPTH, DFF, D])
    out_d = nc.dram_tensor("out", [SEQ, D], F32, kind="ExternalOutput").ap()

    xres = dscr("xres", [T, D], F32)
    yaT_d = dscr("yaT", [A_W, T], BF16)
    ycT_d = dscr("ycT", [CW, T], BF16)
    gT_d = dscr("gT", [3 * D, T], BF16)
    QT_d = dscr("QT", [8, 128, T], BF16)
    KT_d = dscr("KT", [8, 128, T], BF16)
    V_d = dscr("Vd", [8, 128, NT, 128], BF16)
    ybT_d = dscr("ybT", [BW, T], BF16)
    gated_d = dscr("gated", [DFF, T], BF16)

    es = ExitStack()
    cx = Ctx(nc, es)
    PE, ACT, DVE, POOL, SP = cx.pe, cx.act, cx.dve, cx.pool, cx.sp

    def sb(stack, name, shape, dt):
        return stack.enter_context(nc.sbuf_tensor(uid(name), list(shape), dt))

    def pst(stack, name, shape, dt=F32):
        return stack.enter_context(nc.psum_tensor(uid(name), list(shape), dt))

    H = {}
    identf = sb(es, "identf_s", [128, 128], F32)
    identb = sb(es, "identb_s", [128, 128], BF16)
    ccol = sb(es, "ccol_s", [128, 16], F32)
    csil = sb(es, "csil", [128, 16], F32)
    ccolb = sb(es, "ccolb", [128, 16], BF16)
    crep = sb(es, "crep", [128, 16, 128], BF16)
    colp = sb(es, "colp_s", [128, NCOLP], F32)
    rowp = sb(es, "rowp_s", [128, NROWP], F32)
    modc = sb(es, "modc", [128, 8, 8], F32)
    modraw = sb(es, "modraw", [128, 6, 16], F32)
    gate_rep = sb(es, "gate_rep", [128, 4, D], F32)
    lamt = sb(es, "lamt", [128, 8], F32)
    subg = sb(es, "subg", [128, 128], F32)
    r_const = Res()
    r_layer = Res()
    ch_const = cx.chan("const")

    for dst, src in ((identf, identf_d), (identb, identb_d), (ccol, ccol_d)):
        cx.dma(SP, ch_const, dst[:], src, W=[r_const])
    ch_init = cx.chan("init")
    nchunk = max(1, SEQ // 512)
    for i in range(nchunk):
        a, b = i * (SEQ // nchunk), (i + 1) * (SEQ // nchunk)
        cx.dma(SP, ch_init, xres[a:b, :], x_d[a:b, :])
    cx.dma(SP, ch_init, xres[SEQ:T, :], ctx_d)
    cx.op(ACT, lambda: nc.scalar.activation(out=csil[:], in_=ccol[:], func=AF.Silu), R=[r_const], W=[r_const])
    cx.op(DVE, lambda: nc.vector.tensor_copy(out=ccolb[:], in_=csil[:]), R=[r_const], W=[r_const])
    for j in range(16):
        cx.op(DVE, lambda j=j: nc.vector.tensor_copy(out=crep[:, j, :], in_=csil[:, j:j + 1].to_broadcast([128, 128])),
              R=[r_const], W=[r_const])
    cx.barrier()

    def wsrc(w_ap, c0, n):
        return w_ap.rearrange("(k p) n -> p k n", p=128)[:, :, c0:c0 + n]

    def rstd_act(out_ap, ss_ap, scale, R, W):
        cx.op(ACT, lambda: nc.scalar.activation(out=out_ap, in_=ss_ap, func=AF.Ln, bias=EPS, scale=scale), R=R, W=W)
        cx.op(ACT, lambda: nc.scalar.activation(out=out_ap, in_=out_ap, func=AF.Exp, scale=-0.5), R=W, W=W)

    def phase_mod(l):
        lam_init = 0.8 - 0.6 * math.exp(-0.3 * l)
        with ExitStack() as ps:
            wring = Ring(cx, ps, "wada", 2, [128, 8, 1024], BF16, sw=True)
            pcol = pst(ps, "pcol", [128, 512])
            prow = pst(ps, "prow", [128, 2, 512])
            r_pcol, r_prow = Res(), Res()
            tmp = sb(ps, "lamtmp", [128, 4, 64], F32)
            gbias = sb(ps, "gbias_s", [128, 2, D], F32)
            ch = cx.chan(f"lp{l}")
            cx.dma(SP, ch, colp[:], colp_d[l], W=[r_layer])
            cx.dma(SP, ch, rowp[:], rowp_d[l], W=[r_layer])
            cx.dma(SP, ch, gbias[:], gbias_d[l], W=[r_layer])
            awv = ada_w_d[l]
            for i in range(6):
                ws = wring.next()
                cx.dma(POOL, ws.chan, ws.t[:], wsrc(awv, i * 1024, 1024), W=[ws.res])
                if i in (0, 1, 3, 4):
                    for src in range(2):
                        for j in range(8):
                            for kc in range(8):
                                cx.op(PE, lambda src=src, j=j, kc=kc: nc.tensor.matmul(
                                    pcol[:, src * 8 + j:src * 8 + j + 1], ws.t[:, kc, j * 128:(j + 1) * 128],
                                    ccolb[:, src * 8 + kc:src * 8 + kc + 1], start=(kc == 0), stop=(kc == 7)),
                                    R=[ws.res, r_const], W=[r_pcol], inc=(kc == 7))
                    for src in range(2):
                        cx.op(DVE, lambda src=src: nc.vector.tensor_tensor(
                            out=modraw[:, i, src * 8:src * 8 + 8], in0=pcol[:, src * 8:src * 8 + 8],
                            in1=colp[:, CP_ADAB + i * 8:CP_ADAB + i * 8 + 8], op=ALU.add),
                            R=[r_pcol, r_layer], W=[r_layer])
                else:
                    gi = 0 if i == 2 else 2
                    bidx = 0 if i == 2 else 1
                    for src in range(2):
                        for half in range(2):
                            for kc in range(8):
                                cx.op(PE, lambda src=src, half=half, kc=kc: nc.tensor.matmul(
                                    prow[:, half, :], crep[:, src * 8 + kc, :], ws.t[:, kc, half * 512:(half + 1) * 512],
                                    start=(kc == 0), stop=(kc == 7)),
                                    R=[ws.res, r_const], W=[r_prow], inc=(kc == 7))
                        cx.op(DVE, lambda src=src: nc.vector.tensor_tensor(
                            out=gate_rep[:, gi + src, :], in0=prow[:].rearrange("p a b -> p (a b)"),
                            in1=gbias[:, bidx, :], op=ALU.add),
                            R=[r_prow, r_layer], W=[r_layer])
            for src in range(2):
                for n, (isc, ish, cpn) in enumerate(((1, 0, CP_N1), (4, 3, CP_N2))):
                    kG = src * 4 + n * 2
                    cx.op(DVE, lambda: nc.vector.scalar_tensor_tensor(
                        out=modc[:, kG, :], in0=modraw[:, isc, src * 8:src * 8 + 8], scalar=1.0,
                        in1=colp[:, cpn:cpn + 8], op0=ALU.add, op1=ALU.mult), R=[r_layer], W=[r_layer])
                    cx.op(DVE, lambda: nc.vector.tensor_copy(out=modc[:, kG + 1, :], in_=modraw[:, ish, src * 8:src * 8 + 8]),
                          R=[r_layer], W=[r_layer])
            lv = rowp[:, RP_LAM:RP_LAM + 256].rearrange("p (a d) -> p a d", a=4)
            cx.op(DVE, lambda: nc.vector.tensor_tensor(out=tmp[:, 0:2, :], in0=lv[:, 0:4:2, :], in1=lv[:, 1:4:2, :], op=ALU.mult),
                  R=[r_layer], W=[r_layer])
            cx.op(DVE, lambda: nc.vector.tensor_reduce(out=lamt[:, 0:2], in_=tmp[:, 0:2, :], axis=AX.X, op=ALU.add),
                  R=[r_layer], W=[r_layer])
            cx.op(ACT, lambda: nc.scalar.activation(out=lamt[:, 2:4], in_=lamt[:, 0:2], func=AF.Exp), R=[r_layer], W=[r_layer])
            cx.op(DVE, lambda: nc.vector.tensor_tensor(out=lamt[:, 4:5], in0=lamt[:, 2:3], in1=lamt[:, 3:4], op=ALU.subtract),
                  R=[r_layer], W=[r_layer])
            cx.op(DVE, lambda: nc.vector.tensor_scalar(out=lamt[:, 5:6], in0=lamt[:, 4:5], scalar1=float(lam_init), scalar2=None,
                                                        op0=ALU.add), R=[r_layer], W=[r_layer])
            cx.op(DVE, lambda: nc.vector.tensor_scalar(out=subg[:], in0=rowp[:, RP_SUBG:RP_SUBG + 128],
                                                        scalar1=float(1.0 - lam_init), scalar2=None, op0=ALU.mult),
                  R=[r_layer], W=[r_layer])
            cx.barrier()

    def pipeline(n_items, stages):
        S = len(stages)
        for i in range(n_items + S - 1):
            for si in range(S - 1, -1, -1):
                t = i - si
                if 0 <= t < n_items:
                    stages[si](t)

    def phase_norm(which, tiles):
        with ExitStack() as ps:
            xring = Ring(cx, ps, "nx", 5, [128, D], F32)
            xnring = Ring(cx, ps, "nxn", 3, [128, D], F32, chan=False)
            junk = sb(ps, "njunk", [128, D], BF16)
            r_junk = Res()
            stat = Ring(cx, ps, "nst", 6, [128, 2], F32, chan=False)
            pt = [pst(ps, f"npt{i}", [128, 8, 128]) for i in range(2)]
            r_pt = [Res(), Res()]
            S = {}

            def s0(i):
                t = tiles[i]
                xs = xring.next()
                cx.dma(SP, xs.chan, xs.t[:], xres[t * 128:(t + 1) * 128, :], W=[xs.res])
                S[i] = dict(xs=xs)

            def s1(i):
                d = S[i]
                st = d["st"] = stat.next()
                cx.op(ACT, lambda: nc.scalar.activation(out=junk[:], in_=d["xs"].t[:], func=AF.Square, accum_out=st.t[:, 0:1]),
                      R=[d["xs"].res], W=[r_junk, st.res])

            def s2(i):
                st = S[i]["st"]
                rstd_act(st.t[:, 1:2], st.t[:, 0:1], 1.0 / D, R=[st.res], W=[st.res])

            def s3(i):
                d = S[i]
                xs, st = d["xs"], d["st"]
                xn = xnring.next()
                cx.op(ACT, lambda: nc.scalar.activation(out=xn.t[:], in_=xs.t[:], func=AF.Copy, scale=st.t[:, 1:2]),
                      R=[xs.res, st.res], W=[xn.res])
                p, rp = pt[i % 2], r_pt[i % 2]
                d["p"], d["rp"] = p, rp
                for kc in range(8):
                    cx.op(PE, lambda kc=kc: nc.tensor.transpose(out=p[:, kc, :], in_=xn.t[:, kc * 128:(kc + 1) * 128],
                                                               identity=identf[:]),
                          R=[xn.res], W=[rp], inc=(kc == 7))

            def s4(i):
                d = S.pop(i)
                t = tiles[i]
                p, rp = d["p"], d["rp"]
                kG = (0 if t < NTL else 4) + (0 if which == 1 else 2)
                for kc in range(8):
                    cx.op(DVE, lambda kc=kc: nc.vector.tensor_scalar(
                        out=H["hT"][:, kc, t * 128:(t + 1) * 128], in0=p[:, kc, :], scalar1=modc[:, kG, kc:kc + 1],
                        scalar2=modc[:, kG + 1, kc:kc + 1], op0=ALU.mult, op1=ALU.add), R=[rp], W=[])

            pipeline(len(tiles), [s0, s1, s2, s3, s4])
            cx.barrier()

    def phase_A(l, wv):
        with ExitStack() as ps:
            w = sb(ps, "wA", [128, 8, 1024], BF16)
            sgut = sb(ps, "sgut_s", [128, 4, 128], BF16)
            r_w = Res()
            ch = cx.chan(f"wA{l}", sw=True)
            cx.dma(POOL, ch, w[:], wsrc(wv, A_U, 1024), W=[r_w])
            cx.dma(POOL, ch, sgut[:], sgut_d[l], W=[r_w])
            NP = 3
            puv = [pst(ps, f"puv{i}", [128, 1024]) for i in range(NP)]
            r_puv = [Res() for _ in range(NP)]
            pmix = pst(ps, "pmix", [128, 512])
            r_pmix = Res()
            ptr = pst(ps, "ptrA", [128, 4, 128], BF16)
            r_ptr = Res()
            sq = Ring(cx, ps, "Asq", 3, [128, 1024], F32, chan=False)
            guv = Ring(cx, ps, "Aguv", 5, [128, 1024], F32, chan=False)
            stats = Ring(cx, ps, "Ast", 5, [128, 8], F32, chan=False)
            vn = Ring(cx, ps, "Avn", 3, [128, 512], F32, chan=False)
            vln = Ring(cx, ps, "Avln", 3, [128, 512], BF16, chan=False)
            ya = Ring(cx, ps, "Aya", 3, [128, 512], BF16, chan=False)
            junk = sb(ps, "Ajunk", [128, 512], BF16)
            r_junk = Res()
            rows = sb(ps, "Arows", [128, 4, T], BF16)
            r_rows = Res()
            S = {}

            def s0(t):
                p, rp = puv[t % NP], r_puv[t % NP]
                S[t] = dict(p=p, rp=rp)
                for half in range(2):
                    for kc in range(8):
                        cx.op(PE, lambda half=half, kc=kc: nc.tensor.matmul(
                            p[:, half * 512:(half + 1) * 512], H["hT"][:, kc, t * 128:(t + 1) * 128],
                            w[:, kc, half * 512:(half + 1) * 512], start=(kc == 0), stop=(kc == 7)),
                            R=[r_w], W=[rp], inc=(kc == 7))

            def s1(t):
                d = S[t]
                p, rp = d["p"], d["rp"]
                s1_ = d["sq"] = sq.next()
                cx.op(ACT, lambda: nc.scalar.activation(out=s1_.t[:], in_=p[:], func=AF.Square), R=[rp], W=[s1_.res])
                cx.op(DVE, lambda: nc.vector.tensor_scalar(out=s1_.t[:], in0=s1_.t[:], scalar1=0.044715, scalar2=1.0,
                                                            op0=ALU.mult, op1=ALU.add), R=[s1_.res], W=[s1_.res])
                cx.op(DVE, lambda: nc.vector.tensor_tensor(out=s1_.t[:], in0=s1_.t[:], in1=p[:], op=ALU.mult),
                      R=[s1_.res, rp], W=[s1_.res])

            def s2(t):
                d = S[t]
                p, rp, s1_ = d["p"], d["rp"], d["sq"]
                g1 = d["g"] = guv.next()
                st = d["st"] = stats.next()
                v1 = d["vn"] = vn.next()
                cx.op(ACT, lambda: nc.scalar.activation(out=s1_.t[:], in_=s1_.t[:], func=AF.Sigmoid, scale=2.0 * GELU_C),
                      R=[s1_.res], W=[s1_.res])
                cx.op(DVE, lambda: nc.vector.tensor_tensor(out=g1.t[:], in0=s1_.t[:], in1=p[:], op=ALU.mult),
                      R=[s1_.res, rp], W=[g1.res])
                cx.op(DVE, lambda: nc.vector.tensor_reduce(out=st.t[:, 0:1], in_=g1.t[:, 512:1024], axis=AX.X, op=ALU.add),
                      R=[g1.res], W=[st.res])
                cx.op(DVE, lambda: nc.vector.tensor_scalar(out=st.t[:, 1:2], in0=st.t[:, 0:1], scalar1=1.0 / A_W, scalar2=None,
                                                            op0=ALU.mult), R=[st.res], W=[st.res])
                cx.op(DVE, lambda: nc.vector.tensor_scalar(out=v1.t[:], in0=g1.t[:, 512:1024], scalar1=st.t[:, 1:2], scalar2=None,
                                                            op0=ALU.subtract), R=[g1.res, st.res], W=[v1.res])

            def s3(t):
                d = S[t]
                st, v1 = d["st"], d["vn"]
                vl = d["vl"] = vln.next()
                cx.op(ACT, lambda: nc.scalar.activation(out=junk[:], in_=v1.t[:], func=AF.Square, accum_out=st.t[:, 2:3]),
                      R=[v1.res], W=[r_junk, st.res])
                rstd_act(st.t[:, 3:4], st.t[:, 2:3], 1.0 / A_W, R=[st.res], W=[st.res])
                cx.op(DVE, lambda: nc.vector.scalar_tensor_tensor(
                    out=vl.t[:], in0=v1.t[:], scalar=st.t[:, 3:4], in1=rowp[:, RP_LNG:RP_LNG + 512],
                    op0=ALU.mult, op1=ALU.mult), R=[v1.res, st.res, r_layer], W=[vl.res])

            def s4(t):
                d = S[t]
                vl, g1 = d["vl"], d["g"]
                y1 = d["ya"] = ya.next()
                for g in range(4):
                    cx.op(PE, lambda g=g: nc.tensor.matmul(pmix[:, g * 128:(g + 1) * 128], sgut[:, g, :],
                                                           vl.t[:, g * 128:(g + 1) * 128], start=True, stop=True),
                          R=[vl.res, r_w], W=[r_pmix], inc=(g == 3))
                for g in range(4):
                    cx.op(DVE, lambda g=g: nc.vector.scalar_tensor_tensor(
                        out=y1.t[:, g * 128:(g + 1) * 128], in0=pmix[:, g * 128:(g + 1) * 128],
                        scalar=colp[:, CP_SGUB + g:CP_SGUB + g + 1], in1=g1.t[:, g * 128:(g + 1) * 128],
                        op0=ALU.add, op1=ALU.mult), R=[r_pmix, g1.res, r_layer], W=[y1.res])

            def s5(t):
                y1 = S.pop(t)["ya"]
                for j in range(4):
                    cx.op(PE, lambda j=j: nc.tensor.transpose(out=ptr[:, j, :], in_=y1.t[:, j * 128:(j + 1) * 128],
                                                             identity=identb[:]), R=[y1.res], W=[r_ptr], inc=(j == 3))
                cx.op(ACT, lambda: nc.scalar.copy(out=rows[:, :, t * 128:(t + 1) * 128], in_=ptr[:]), R=[r_ptr], W=[r_rows])

            pipeline(NT, [s0, s1, s2, s3, s4, s5])
            cx.dma(SP, cx.chan("yaTst"), yaT_d.rearrange("(k p) t -> p k t", p=128), rows[:], R=[r_rows])
            cx.barrier()

    def phase_C(l, wv):
        with ExitStack() as ps:
            w = sb(ps, "wC", [128, 8, 3 * CW], BF16)
            r_w = Res()
            ch = cx.chan(f"wC{l}", sw=True)
            cx.dma(POOL, ch, w[:], wsrc(wv, C_IN, 3 * CW), W=[r_w])
            pp = [[pst(ps, f"pC{b}_{i}", [128, 512]) for i in range(3)] for b in range(2)]
            r_pp = [[Res() for _ in range(3)] for _ in range(2)]
            xin_s = Ring(cx, ps, "Cxin", 2, [128, 512], F32, chan=False)
            prow = Ring(cx, ps, "Cp", 2, [128, T], F32, chan=False)
            gbrow = Ring(cx, ps, "Cgb", 2, [128, T], BF16)
            acc = sb(ps, "Cacc", [128, T], F32)
            r_acc = Res()
            it = 0
            for j in range(4):
                pr = prow.next()
                gb = gbrow.next()
                for (g0, n) in groups:
                    P = pp[it % 2]
                    RP = r_pp[it % 2]
                    it += 1
                    for part in range(3):
                        for kc in range(8):
                            cx.op(PE, lambda part=part, kc=kc: nc.tensor.matmul(
                                P[part][:, 0:n], w[:, kc, part * CW + j * 128:part * CW + (j + 1) * 128],
                                H["hT"][:, kc, g0:g0 + n], start=(kc == 0), stop=(kc == 7)),
                                R=[r_w], W=[RP[part]], inc=(kc == 7))
                    xs = xin_s.next()
                    cx.op(ACT, lambda: nc.scalar.copy(out=xs.t[:, 0:n], in_=P[2][:, 0:n]), R=[RP[2]], W=[xs.res])
                    cx.op(DVE, lambda: nc.vector.tensor_tensor(out=pr.t[:, g0:g0 + n], in0=P[1][:, 0:n], in1=xs.t[:, 0:n],
                                                                op=ALU.mult), R=[RP[1], xs.res], W=[pr.res])
                    cx.op(ACT, lambda: nc.scalar.copy(out=gb.t[:, g0:g0 + n], in_=P[0][:, 0:n]), R=[RP[0]], W=[gb.res])
                wc = lambda tap: colp[:, CP_CONVW + tap * 4 + j:CP_CONVW + tap * 4 + j + 1]
                for (a, b) in ranges:
                    cx.op(DVE, lambda: nc.vector.tensor_scalar(out=acc[:, a:b], in0=pr.t[:, a:b], scalar1=wc(1), scalar2=None,
                                                                op0=ALU.mult), R=[pr.res, r_layer], W=[r_acc])
                    cx.op(DVE, lambda: nc.vector.scalar_tensor_tensor(
                        out=acc[:, a + 1:b], in0=pr.t[:, a:b - 1], scalar=wc(0), in1=acc[:, a + 1:b],
                        op0=ALU.mult, op1=ALU.add), R=[pr.res, r_acc], W=[r_acc])
                    cx.op(DVE, lambda: nc.vector.scalar_tensor_tensor(
                        out=acc[:, a:b - 1], in0=pr.t[:, a + 1:b], scalar=wc(2), in1=acc[:, a:b - 1],
                        op0=ALU.mult, op1=ALU.add), R=[pr.res, r_acc], W=[r_acc])
                cx.op(DVE, lambda: nc.vector.tensor_tensor(out=gb.t[:], in0=acc[:], in1=gb.t[:], op=ALU.mult),
                      R=[r_acc, gb.res], W=[gb.res])
                cx.dma(SP, gb.chan, ycT_d[j * 128:(j + 1) * 128, :], gb.t[:], R=[gb.res])
            cx.barrier()

    def phase_G(l, wv):
        with ExitStack() as ps:
            wring = Ring(cx, ps, "wG", 2, [128, 8, 1024], BF16, sw=True)
            pg = [pst(ps, f"pG{i}", [128, 512]) for i in range(4)]
            r_pg = [Res() for _ in range(4)]
            grow = Ring(cx, ps, "Grow", 3, [128, T], BF16)
            it = 0
            for blk in range(3):
                ws = wring.next()
                cx.dma(POOL, ws.chan, ws.t[:], wsrc(wv, GATE + blk * 1024, 1024), W=[ws.res])
                for jj in range(8):
                    gr = grow.next()
                    for (g0, n) in groups:
                        P, RP = pg[it % 4], r_pg[it % 4]
                        it += 1
                        for kc in range(8):
                            cx.op(PE, lambda kc=kc: nc.tensor.matmul(
                                P[:, 0:n], ws.t[:, kc, jj * 128:(jj + 1) * 128], H["hT"][:, kc, g0:g0 + n],
                                start=(kc == 0), stop=(kc == 7)), R=[ws.res], W=[RP], inc=(kc == 7))
                        cx.op(ACT, lambda: nc.scalar.activation(out=gr.t[:, g0:g0 + n], in_=P[:, 0:n], func=AF.Sigmoid),
                              R=[RP], W=[gr.res])
                    j = blk * 8 + jj
                    cx.dma(SP, gr.chan, gT_d[j * 128:(j + 1) * 128, :], gr.t[:], R=[gr.res])
            cx.barrier()

    def phase_V(l, wv):
        with ExitStack() as ps:
            w = sb(ps, "wV", [128, 8, 1024], BF16)
            r_w = Res()
            ch = cx.chan(f"wV{l}", sw=True)
            cx.dma(POOL, ch, w[:], wsrc(wv, B_V, 1024), W=[r_w])
            pv = [pst(ps, f"pV{i}", [128, 1024]) for i in range(2)]
            r_pv = [Res(), Res()]
            stg = Ring(cx, ps, "Vst", 3, [128, 8, 128], BF16)
            vdst = V_d.rearrange("h p t e -> p h t e")
            for t in range(NT):
                P, RP = pv[t % 2], r_pv[t % 2]
                for half in range(2):
                    for kc in range(8):
                        cx.op(PE, lambda half=half, kc=kc: nc.tensor.matmul(
                            P[:, half * 512:(half + 1) * 512], H["hT"][:, kc, t * 128:(t + 1) * 128],
                            w[:, kc, half * 512:(half + 1) * 512], start=(kc == 0), stop=(kc == 7)),
                            R=[r_w], W=[RP], inc=(kc == 7))
                s = stg.next()
                eng = ACT if t % 2 == 0 else DVE
                if eng is ACT:
                    cx.op(ACT, lambda: nc.scalar.copy(out=s.t[:].rearrange("p h e -> p (h e)"), in_=P[:]), R=[RP], W=[s.res])
                else:
                    cx.op(DVE, lambda: nc.vector.tensor_copy(out=s.t[:].rearrange("p h e -> p (h e)"), in_=P[:]), R=[RP], W=[s.res])
                cx.dma(SP, s.chan, vdst[:, :, t, :], s.t[:], R=[s.res])
            cx.barrier()

    def phase_QK(l, wv):
        with ExitStack() as ps:
            w = sb(ps, "wQK", [128, 8, 2048], BF16)
            r_w = Res()
            ch = cx.chan(f"wQKt{l}")
            chw = cx.chan(f"wQK{l}", sw=True)
            r_tab = Res()
            cx.dma(POOL, chw, w[:, :, 0:1024], wsrc(wv, B_Q, 1024), W=[r_w])
            cx.dma(POOL, chw, w[:, :, 1024:2048], wsrc(wv, B_K, 1024), W=[r_w])
            cost = sb(ps, "cost_s", [128, NT, 32], F32)
            sint = sb(ps, "sint_s", [128, NT, 32], F32)
            cx.dma(SP, ch, cost[:], cos_d, W=[r_tab])
            cx.dma(SP, ch, sint[:], sin_d, W=[r_tab])
            NP = 3
            pq = [pst(ps, f"pQK{i}", [128, 1024]) for i in range(NP)]
            r_pq = [Res() for _ in range(NP)]
            ptr = [pst(ps, f"ptrQK{i}", [128, 8, 128], BF16) for i in range(1)]
            r_ptr = [Res()]
            sq = Ring(cx, ps, "QKsq", 2, [128, 1024], F32, chan=False)
            st = Ring(cx, ps, "QKst", 5, [128, 32], F32, chan=False)
            qn = Ring(cx, ps, "QKqn", 4, [128, 1024], F32, chan=False)
            tA = Ring(cx, ps, "QKtA", 2, [128, 512], F32, chan=False)
            tB = Ring(cx, ps, "QKtB", 2, [128, 512], F32, chan=False)
            tC = Ring(cx, ps, "QKtC", 2, [128, 512], F32, chan=False)
            tD = Ring(cx, ps, "QKtD", 2, [128, 512], F32, chan=False)
            qo = Ring(cx, ps, "QKqo", 3, [128, 1024], BF16, chan=False)
            stg = Ring(cx, ps, "QKstg", 3, [128, 8, 128], BF16)
            dsts = (QT_d.rearrange("c p t -> p c t"), KT_d.rearrange("c p t -> p c t"))
            S = {}

            def s0(i):
                t, hf = i // 2, i % 2
                p, rp = pq[i % NP], r_pq[i % NP]
                S[i] = dict(p=p, rp=rp)
                for blk in range(2):
                    for kc in range(8):
                        cx.op(PE, lambda blk=blk, kc=kc: nc.tensor.matmul(
                            p[:, blk * 512:(blk + 1) * 512], H["hT"][:, kc, t * 128:(t + 1) * 128],
                            w[:, kc, hf * 1024 + blk * 512:hf * 1024 + (blk + 1) * 512], start=(kc == 0), stop=(kc == 7)),
                            R=[r_w], W=[rp], inc=(kc == 7))

            def s1(i):
                d = S[i]
                p, rp = d["p"], d["rp"]
                sq1 = sq.next()
                s = d["st"] = st.next()
                cx.op(ACT, lambda: nc.scalar.activation(out=sq1.t[:], in_=p[:], func=AF.Square), R=[rp], W=[sq1.res])
                cx.op(DVE, lambda: nc.vector.tensor_reduce(out=s.t[:, 0:16], in_=sq1.t[:].rearrange("p (h d) -> p h d", d=64),
                                                            axis=AX.X, op=ALU.add), R=[sq1.res], W=[s.res])

            def s2(i):
                d = S[i]
                hf = i % 2
                p, rp, s = d["p"], d["rp"], d["st"]
                q1 = d["qn"] = qn.next()
                rstd_act(s.t[:, 16:32], s.t[:, 0:16], 1.0 / 64, R=[s.res], W=[s.res])
                cx.op(DVE, lambda: nc.vector.tensor_tensor(
                    out=q1.t[:].rearrange("p (h d) -> p h d", d=64), in0=p[:].rearrange("p (h d) -> p h d", d=64),
                    in1=s.t[:, 16:32].unsqueeze(2).to_broadcast([128, 16, 64]), op=ALU.mult),
                    R=[rp, s.res], W=[q1.res])
                gq = rowp[:, RP_QG + hf * 64:RP_QG + (hf + 1) * 64].unsqueeze(1).to_broadcast([128, 16, 64])
                cx.op(POOL, lambda: nc.gpsimd.tensor_tensor(
                    out=q1.t[:].rearrange("p (h d) -> p h d", d=64), in0=q1.t[:].rearrange("p (h d) -> p h d", d=64),
                    in1=gq, op=ALU.mult), R=[q1.res, r_layer], W=[q1.res])

            def s3(i):
                d = S[i]
                t = i // 2
                q1 = d["qn"]
                o1 = d["qo"] = qo.next()
                a1, b1, c1, d1 = tA.next(), tB.next(), tC.next(), tD.next()
                v5 = q1.t[:].rearrange("p (h b s f) -> p h b s f", h=16, b=2, s=2)
                x1, x2 = v5[:, :, :, 0, :], v5[:, :, :, 1, :]
                cs = cost[:, t, :].rearrange("p (b f) -> p b f", b=2).unsqueeze(1).to_broadcast([128, 16, 2, 16])
                sn = sint[:, t, :].rearrange("p (b f) -> p b f", b=2).unsqueeze(1).to_broadcast([128, 16, 2, 16])
                o5 = o1.t[:].rearrange("p (h b s f) -> p h b s f", h=16, b=2, s=2)
                v4 = lambda tt: tt.t[:].rearrange("p (h b f) -> p h b f", h=16, b=2)
                cx.op(DVE, lambda: nc.vector.tensor_tensor(out=v4(a1), in0=x1, in1=cs, op=ALU.mult), R=[q1.res, r_tab], W=[a1.res])
                cx.op(POOL, lambda: nc.gpsimd.tensor_tensor(out=v4(c1), in0=x1, in1=sn, op=ALU.mult), R=[q1.res, r_tab], W=[c1.res])
                cx.op(DVE, lambda: nc.vector.tensor_tensor(out=v4(b1), in0=x2, in1=sn, op=ALU.mult), R=[q1.res, r_tab], W=[b1.res])
                cx.op(POOL, lambda: nc.gpsimd.tensor_tensor(out=v4(d1), in0=x2, in1=cs, op=ALU.mult), R=[q1.res, r_tab], W=[d1.res])
                cx.op(DVE, lambda: nc.vector.tensor_tensor(out=o5[:, :, :, 0, :], in0=v4(a1), in1=v4(b1), op=ALU.subtract),
                      R=[a1.res, b1.res], W=[o1.res])
                cx.op(POOL, lambda: nc.gpsimd.tensor_tensor(out=o5[:, :, :, 1, :], in0=v4(c1), in1=v4(d1), op=ALU.add),
                      R=[c1.res, d1.res], W=[o1.res])

            def s4(i):
                d = S.pop(i)
                t, hf = i // 2, i % 2
                o1 = d["qo"]
                pt_, rpt = ptr[0], r_ptr[0]
                sg = stg.next()
                for c in range(8):
                    cx.op(PE, lambda c=c: nc.tensor.transpose(out=pt_[:, c, :], in_=o1.t[:, c * 128:(c + 1) * 128],
                                                             identity=identb[:]), R=[o1.res], W=[rpt], inc=(c == 7))
                cx.op(ACT, lambda: nc.scalar.copy(out=sg.t[:], in_=pt_[:]), R=[rpt], W=[sg.res])
                cx.dma(SP, sg.chan, dsts[hf][:, :, t * 128:(t + 1) * 128], sg.t[:], R=[sg.res])

            pipeline(2 * NT, [s0, s1, s2, s3, s4])
            cx.barrier()

    def phase_attn(l, do_ctx):
        with ExitStack() as ps:
            ktr = Ring(cx, ps, "aK", 1, [128, 2, T], BF16)
            qpr = Ring(cx, ps, "aQ", 1, [128, 2, 2, T], BF16)
            vring = Ring(cx, ps, "aV", 2, [128, NT, 132], BF16)
            pS = [pst(ps, f"aS{i}", [128, 2, 512]) for i in range(2)]
            r_pS = [Res(), Res()]
            pO = pst(ps, "aO", [128, 3, 512])
            r_pO = Res()
            ptr = pst(ps, "aT", [128, 128], BF16)
            r_ptr = Res()
            pb = Ring(cx, ps, "aP", 3, [128, 2, 512], BF16, chan=False)
            est = Ring(cx, ps, "aE", 10, [128, 16], F32, chan=False)
            t1r = Ring(cx, ps, "aT1", 2, [128, 128], F32, chan=False)
            atr = Ring(cx, ps, "aAt", 10, [128, 128], F32, chan=False)
            ybr = Ring(cx, ps, "aYb", 10, [128, 128], BF16, chan=False)
            junk = sb(ps, "aJunk", [128, 128], BF16)
            r_junk = Res()
            yrow = Ring(cx, ps, "aYrow", 2, [128, T], BF16)
            r_z = Res()
            for s in qpr.slots:
                cx.op(DVE, lambda s=s: nc.vector.memset(s.t[:], 0.0), W=[s.res])
            for s in vring.slots:
                cx.op(POOL, lambda s=s: nc.gpsimd.memset(s.t[:, :, 128:129], 1.0), W=[s.res])

            def oacc(a, qi):
                if qi < 3:
                    return pO[:, a, qi * 129:(qi + 1) * 129]
                return pO[:, 2, a * 129:(a + 1) * 129]

            itc = [0]
            ocopy = Ring(cx, ps, "aOc", 2, [128, 3, 512], F32, chan=False)

            def ocv(oc, a, qi):
                if qi < 3:
                    return oc.t[:, a, qi * 129:(qi + 1) * 129]
                return oc.t[:, 2, a * 129:(a + 1) * 129]

            for hp in range(4):
                ks = ktr.next()
                qs = qpr.next()
                for a in range(2):
                    c = a * 4 + hp
                    cx.dma(SP, ks.chan, ks.t[:, a, :], KT_d[c], W=[ks.res])
                    for hh in range(2):
                        cx.dma(SP, qs.chan, qs.t[hh * 64:(hh + 1) * 64, a, hh, :], QT_d[c, hh * 64:(hh + 1) * 64, :], W=[qs.res])
                for hh in range(2):
                    h = hp * 2 + hh
                    vs = vring.next()
                    cx.dma(SP, vs.chan, vs.t[:, :, 0:128], V_d[h], W=[vs.res])
                    yr = yrow.next()
                    qgroups = groups if do_ctx else groups[:-1]
                    steps = []
                    for (g0, n) in qgroups:
                        kts = list(range(NT)) if g0 < SEQ else list(range(NTL, NT))
                        for ki, kt in enumerate(kts):
                            steps.append((g0, n, ki, kt, ki == len(kts) - 1))

                    def emit_qk(step):
                        g0, n, ki, kt, last = step
                        P, RP = pS[itc[0] % 2], r_pS[itc[0] % 2]
                        itc[0] += 1
                        for a in range(2):
                            cx.op(PE, lambda a=a: nc.tensor.matmul(
                                P[:, a, 0:n], ks.t[:, a, kt * 128:(kt + 1) * 128], qs.t[:, a, hh, g0:g0 + n],
                                start=True, stop=True), R=[ks.res, qs.res], W=[RP], inc=(a == 1))
                        return P, RP

                    deferred = []
                    qk_out = {0: emit_qk(steps[0])}
                    if len(steps) > 1:
                        qk_out[1] = emit_qk(steps[1])
                    first_in_bank = {0: True, 1: True, 2: True}
                    for si, step in enumerate(steps):
                        g0, n, ki, kt, last = step
                        nq = n // 128
                        P, RP = qk_out.pop(si)
                        if ki == 0:
                            first_in_bank = {0: True, 1: True, 2: True}
                        pbs = pb.next()
                        cx.op(ACT, lambda: nc.scalar.activation(out=pbs.t[:, :, 0:n], in_=P[:, :, 0:n], func=AF.Exp, scale=0.125),
                              R=[RP], W=[pbs.res])
                        if si + 2 < len(steps):
                            qk_out[si + 2] = emit_qk(steps[si + 2])
                        for a in range(2):
                            for qi in range(nq):
                                bank = a if qi < 3 else 2
                                st_flag = first_in_bank[bank] and ki == 0
                                first_in_bank[bank] = False
                                cx.op(PE, lambda a=a, qi=qi, st_flag=st_flag: nc.tensor.matmul(
                                    oacc(a, qi), pbs.t[:, a, qi * 128:(qi + 1) * 128], vs.t[:, kt, 0:129],
                                    start=st_flag, stop=last, skip_group_check=True),
                                    R=[pbs.res, vs.res], W=[r_pO], inc=(a == 1 and qi == nq - 1))
                        if deferred and not last:
                            deferred.pop(0)()
                        if not last:
                            continue
                        oc = ocopy.next()
                        for b in range(3):
                            wcols = 387 if b < 2 else 258
                            cx.op(DVE, lambda b=b, wcols=wcols: nc.vector.tensor_copy(out=oc.t[:, b, 0:wcols], in_=pO[:, b, 0:wcols]),
                                  R=[r_pO], W=[oc.res])
                        stA, stB, stC = [], [], []
                        for qi in range(nq):
                            e = est.next()
                            at, yb = atr.next(), ybr.next()
                            o0, o1 = ocv(oc, 0, qi), ocv(oc, 1, qi)
                            q0 = g0 + qi * 128

                            def fA(e=e, at=at, o0=o0, o1=o1, oc=oc):
                                t1 = t1r.next()
                                cx.op(DVE, lambda: nc.vector.reciprocal(out=e.t[:, 0:1], in_=o0[:, 128:129]), R=[oc.res], W=[e.res])
                                cx.op(DVE, lambda: nc.vector.reciprocal(out=e.t[:, 1:2], in_=o1[:, 128:129]), R=[oc.res], W=[e.res])
                                cx.op(DVE, lambda: nc.vector.tensor_tensor(out=e.t[:, 2:3], in0=e.t[:, 1:2], in1=lamt[:, 5:6], op=ALU.mult),
                                      R=[e.res, r_layer], W=[e.res])
                                cx.op(DVE, lambda: nc.vector.tensor_scalar(out=t1.t[:], in0=o1[:, 0:128], scalar1=e.t[:, 2:3], scalar2=None,
                                                                            op0=ALU.mult), R=[oc.res, e.res], W=[t1.res])
                                cx.op(DVE, lambda: nc.vector.scalar_tensor_tensor(
                                    out=at.t[:], in0=o0[:, 0:128], scalar=e.t[:, 0:1], in1=t1.t[:], op0=ALU.mult, op1=ALU.subtract),
                                    R=[oc.res, e.res, t1.res], W=[at.res])

                            def fB(e=e, at=at, yb=yb):
                                cx.op(ACT, lambda: nc.scalar.activation(out=junk[:], in_=at.t[:], func=AF.Square, accum_out=e.t[:, 3:4]),
                                      R=[at.res], W=[r_junk, e.res])
                                rstd_act(e.t[:, 4:5], e.t[:, 3:4], 1.0 / 128, R=[e.res], W=[e.res])
                                cx.op(DVE, lambda: nc.vector.scalar_tensor_tensor(
                                    out=yb.t[:], in0=at.t[:], scalar=e.t[:, 4:5], in1=subg[:], op0=ALU.mult, op1=ALU.mult),
                                    R=[at.res, e.res, r_layer], W=[yb.res])

                            def fC(yb=yb, q0=q0):
                                cx.op(PE, lambda: nc.tensor.transpose(out=ptr[:], in_=yb.t[:], identity=identb[:]), R=[yb.res], W=[r_ptr])
                                cx.op(DVE, lambda: nc.vector.tensor_copy(out=yr.t[:, q0:q0 + 128], in_=ptr[:]), R=[r_ptr], W=[yr.res])
                            stA.append(fA)
                            stB.append(fB)
                            stC.append(fC)
                        deferred.extend(stA + stB + stC)
                    for fn in deferred:
                        fn()
                    deferred = []
                    hi = T if do_ctx else SEQ
                    cx.dma(SP, yr.chan, ybT_d[h * 128:(h + 1) * 128, 0:hi], yr.t[:, 0:hi], R=[yr.res])
            cx.barrier()

    def resid_update(P, RP, xs, xo, gidx, dst_ap):
        cx.op(DVE, lambda: nc.vector.tensor_tensor(out=xo.t[:], in0=P[:], in1=gate_rep[:, gidx, :], op=ALU.mult),
              R=[RP, r_layer], W=[xo.res])
        cx.op(POOL, lambda: nc.gpsimd.tensor_tensor(out=xo.t[:], in0=xo.t[:], in1=xs.t[:], op=ALU.add),
              R=[xo.res, xs.res], W=[xo.res])
        cx.dma(SP, xo.chan, dst_ap, xo.t[:], R=[xo.res])

    def merge_weights(l, stack):
        wa = sb(stack, "wa", [128, 4, D], BF16)
        wb = sb(stack, "wb", [128, 8, D], BF16)
        wc = sb(stack, "wc", [128, 4, D], BF16)
        wo = sb(stack, "wo", [128, 8, D], BF16)
        rs = [Res() for _ in range(4)]
        for wt, src, r in ((wa, w_a_d, rs[0]), (wb, w_b_d, rs[1]), (wc, w_c_d, rs[2]), (wo, w_o_d, rs[3])):
            cx.dma(POOL, cx.chan(f"wM{l}", sw=True), wt[:], wsrc(src[l], 0, D), W=[r])
        return (wa, wb, wc, wo), rs

    def phase_merge(l, do_ctx, mw):
        with ExitStack() as ps:
            (wa, wb, wc, wo), (r_wa, r_wb, r_wc, r_wo) = mw
            inr = Ring(cx, ps, "mIn", 2, [128, 40, 512], BF16)
            pbr = [[pst(ps, f"mP{b}_{i}", [128, 512]) for i in range(3)] for b in range(2)]
            r_pbr = [[Res() for _ in range(3)] for _ in range(2)]
            po = [pst(ps, f"mO{i}", [128, 1024]) for i in range(1)]
            r_po = [Res()]
            ta = Ring(cx, ps, "mTa", 2, [128, 512], F32, chan=False)
            tb = Ring(cx, ps, "mTb", 2, [128, 512], F32, chan=False)
            tc = Ring(cx, ps, "mTc", 2, [128, 512], F32, chan=False)
            mT = Ring(cx, ps, "mMT", 2, [128, 8, 512], BF16, chan=False)
            xin = Ring(cx, ps, "mX", 3, [128, D], F32)
            xout = Ring(cx, ps, "mXo", 2, [128, D], F32)
            gl = groups if do_ctx else groups[:-1]
            ins = {}

            def load(gi):
                g0, n = gl[gi]
                s = inr.next()
                cx.dma(SP, s.chan, s.t[:, 0:4, 0:n], yaT_d.rearrange("(k p) t -> p k t", p=128)[:, :, g0:g0 + n], W=[s.res])
                cx.dma(SP, s.chan, s.t[:, 4:12, 0:n], ybT_d.rearrange("(k p) t -> p k t", p=128)[:, :, g0:g0 + n], W=[s.res])
                cx.dma(SP, s.chan, s.t[:, 12:16, 0:n], ycT_d.rearrange("(k p) t -> p k t", p=128)[:, :, g0:g0 + n], W=[s.res])
                cx.dma(SP, s.chan, s.t[:, 16:40, 0:n], gT_d.rearrange("(k p) t -> p k t", p=128)[:, :, g0:g0 + n], W=[s.res])
                ins[gi] = s

            load(0)
            ot = 0
            for gi, (g0, n) in enumerate(gl):
                if gi + 1 < len(gl):
                    load(gi + 1)
                s = ins.pop(gi)
                m = mT.next()
                xl = []
                for qi in range(n // 128):
                    xs = xin.next() if qi < 3 else None
                    if xs is not None:
                        tix = g0 // 128 + qi
                        cx.dma(SP, xs.chan, xs.t[:], xres[tix * 128:(tix + 1) * 128, :], W=[xs.res])
                    xl.append(xs)
                for oc in range(8):
                    P, RP = pbr[oc % 2], r_pbr[oc % 2]
                    for br, (wt, nk, off, rwt) in enumerate(((wa, 4, 0, r_wa), (wb, 8, 4, r_wb), (wc, 4, 12, r_wc))):
                        for kc in range(nk):
                            cx.op(PE, lambda br=br, wt=wt, kc=kc, off=off, nk=nk: nc.tensor.matmul(
                                P[br][:, 0:n], wt[:, kc, oc * 128:(oc + 1) * 128], s.t[:, off + kc, 0:n],
                                start=(kc == 0), stop=(kc == nk - 1)), R=[rwt, s.res], W=[RP[br]], inc=(kc == nk - 1))
                    a1, b1, c1 = ta.next(), tb.next(), tc.next()
                    cx.op(DVE, lambda: nc.vector.tensor_tensor(out=a1.t[:, 0:n], in0=P[0][:, 0:n], in1=s.t[:, 16 + oc, 0:n], op=ALU.mult),
                          R=[RP[0], s.res], W=[a1.res])
                    cx.op(DVE, lambda: nc.vector.tensor_tensor(out=b1.t[:, 0:n], in0=P[1][:, 0:n], in1=s.t[:, 24 + oc, 0:n], op=ALU.mult),
                          R=[RP[1], s.res], W=[b1.res])
                    cx.op(DVE, lambda: nc.vector.tensor_tensor(out=c1.t[:, 0:n], in0=P[2][:, 0:n], in1=s.t[:, 32 + oc, 0:n], op=ALU.mult),
                          R=[RP[2], s.res], W=[c1.res])
                    cx.op(POOL, lambda: nc.gpsimd.tensor_tensor(out=a1.t[:, 0:n], in0=a1.t[:, 0:n], in1=b1.t[:, 0:n], op=ALU.add),
                          R=[a1.res, b1.res], W=[a1.res])
                    cx.op(POOL, lambda: nc.gpsimd.tensor_tensor(out=m.t[:, oc, 0:n], in0=a1.t[:, 0:n], in1=c1.t[:, 0:n], op=ALU.add),
                          R=[a1.res, c1.res], W=[m.res])
                for qi in range(n // 128):
                    tix = g0 // 128 + qi
                    xs = xl[qi]
                    if xs is None:
                        xs = xin.next()
                        cx.dma(SP, xs.chan, xs.t[:], xres[tix * 128:(tix + 1) * 128, :], W=[xs.res])
                    P, RP = po[0], r_po[0]
                    ot += 1
                    for half in range(2):
                        for kc in range(8):
                            cx.op(PE, lambda half=half, kc=kc: nc.tensor.matmul(
                                P[:, half * 512:(half + 1) * 512], m.t[:, kc, qi * 128:(qi + 1) * 128],
                                wo[:, kc, half * 512:(half + 1) * 512], start=(kc == 0), stop=(kc == 7)),
                                R=[m.res, r_wo], W=[RP], inc=(kc == 7))
                    xo = xout.next()
                    resid_update(P, RP, xs, xo, 0 if tix < NTL else 1, xres[tix * 128:(tix + 1) * 128, :])
            cx.barrier()

    def phase_ffn_up(l, do_ctx):
        with ExitStack() as ps:
            wring = Ring(cx, ps, "wF", 2, [128, 8, 1024], BF16, sw=True)
            pf = [[pst(ps, f"pF{b}_{i}", [128, 512]) for i in range(2)] for b in range(2)]
            r_pf = [[Res(), Res()] for _ in range(2)]
            raw = Ring(cx, ps, "Fraw", 2, [128, 2, T], BF16)
            accg = sb(ps, "Faccg", [128, T], F32)
            accv = sb(ps, "Faccv", [128, T], F32)
            r_ag, r_av = Res(), Res()
            fup = f_up_d[l]
            gl = groups if do_ctx else groups[:-1]
            rl = ranges if do_ctx else ranges[:1]
            hi = T if do_ctx else SEQ
            it = 0
            pend = []
            NCH = DFF // 128
            for blk in range((NCH + 3) // 4):
                j0 = blk * 4
                nj = min(4, NCH - j0)
                ws = wring.next()
                cx.dma(POOL, ws.chan, ws.t[:, :, 0:nj * 128], wsrc(fup, j0 * 128, nj * 128), W=[ws.res])
                cx.dma(POOL, ws.chan, ws.t[:, :, 512:512 + nj * 128], wsrc(fup, DFF + j0 * 128, nj * 128), W=[ws.res])
                for jj in range(nj):
                    j = j0 + jj
                    rw = raw.next()
                    for gidx, (g0, n) in enumerate(gl):
                        if gidx == min(4, len(gl) - 1):
                            for fn in pend:
                                fn()
                            pend = []
                        P, RP = pf[it % 2], r_pf[it % 2]
                        it += 1
                        for part in range(2):
                            for kc in range(8):
                                cx.op(PE, lambda part=part, kc=kc: nc.tensor.matmul(
                                    P[part][:, 0:n], ws.t[:, kc, part * 512 + jj * 128:part * 512 + (jj + 1) * 128],
                                    H["hT"][:, kc, g0:g0 + n], start=(kc == 0), stop=(kc == 7)),
                                    R=[ws.res], W=[RP[part]], inc=(kc == 7))
                        cx.op(ACT, lambda: nc.scalar.copy(out=rw.t[:, 0, g0:g0 + n], in_=P[0][:, 0:n]), R=[RP[0]], W=[rw.res])
                        cx.op(ACT, lambda: nc.scalar.copy(out=rw.t[:, 1, g0:g0 + n], in_=P[1][:, 0:n]), R=[RP[1]], W=[rw.res])
                    for part, (acc, racc, eng, eh) in enumerate(((accg, r_ag, DVE, nc.vector), (accv, r_av, DVE, nc.vector))):
                        cj = part * NCH + j
                        wcol = lambda tap, cj=cj: colp[:, CP_FCW + tap * 44 + cj:CP_FCW + tap * 44 + cj + 1]
                        bcol = colp[:, CP_FCB + cj:CP_FCB + cj + 1]
                        for (a, b) in rl:
                            cx.op(eng, lambda: eh.tensor_scalar(
                                out=acc[:, a:b], in0=rw.t[:, part, a:b], scalar1=wcol(1), scalar2=bcol,
                                op0=ALU.mult, op1=ALU.add), R=[rw.res, r_layer], W=[racc])
                            cx.op(eng, lambda: eh.scalar_tensor_tensor(
                                out=acc[:, a + 1:b], in0=rw.t[:, part, a:b - 1], scalar=wcol(0), in1=acc[:, a + 1:b],
                                op0=ALU.mult, op1=ALU.add), R=[rw.res, racc], W=[racc])
                            cx.op(eng, lambda: eh.scalar_tensor_tensor(
                                out=acc[:, a:b - 1], in0=rw.t[:, part, a + 1:b], scalar=wcol(2), in1=acc[:, a:b - 1],
                                op0=ALU.mult, op1=ALU.add), R=[rw.res, racc], W=[racc])

                    def tail(rw=rw, j=j):
                        cx.op(ACT, lambda: nc.scalar.activation(out=accg[:, 0:hi], in_=accg[:, 0:hi], func=AF.Silu), R=[r_ag], W=[r_ag])
                        cx.op(POOL, lambda: nc.gpsimd.tensor_tensor(out=rw.t[:, 0, 0:hi], in0=accg[:, 0:hi], in1=accv[:, 0:hi], op=ALU.mult),
                              R=[r_ag, r_av, rw.res], W=[rw.res])
                        cx.dma(SP, rw.chan, gated_d[j * 128:(j + 1) * 128, 0:hi], rw.t[:, 0, 0:hi], R=[rw.res])
                    pend.append(tail)
            for fn in pend:
                fn()
            cx.barrier()

    def phase_ffn_down(l, do_ctx, final):
        with ExitStack() as ps:
            NCH = DFF // 128
            wd = sb(ps, "wd", [128, NCH, D], BF16)
            r_wd = [Res(), Res()]
            for i in range(2):
                cx.dma(POOL, cx.chan(f"wD{l}", sw=True), wd[:, i * 11:(i + 1) * 11, :],
                       f_dn_d[l].rearrange("(k p) n -> p k n", p=128)[:, i * 11:(i + 1) * 11, :], W=[r_wd[i]])
            inr = Ring(cx, ps, "dIn", 2, [128, NCH, 512], BF16)
            po = [pst(ps, f"dO{i}", [128, 1024]) for i in range(2)]
            r_po = [Res(), Res()]
            xin = Ring(cx, ps, "dX", 3, [128, D], F32)
            xout = Ring(cx, ps, "dXo", 2, [128, D], F32)
            gl = groups if do_ctx else groups[:-1]
            ins = {}

            def load(gi):
                g0, n = gl[gi]
                s = inr.next()
                cx.dma(SP, s.chan, s.t[:, :, 0:n], gated_d.rearrange("(k p) t -> p k t", p=128)[:, :, g0:g0 + n], W=[s.res])
                ins[gi] = s

            load(0)
            ot = 0
            for gi, (g0, n) in enumerate(gl):
                if gi + 1 < len(gl):
                    load(gi + 1)
                s = ins.pop(gi)
                for qi in range(n // 128):
                    tix = g0 // 128 + qi
                    xs = xin.next()
                    cx.dma(SP, xs.chan, xs.t[:], xres[tix * 128:(tix + 1) * 128, :], W=[xs.res])
                    P, RP = po[ot % 2], r_po[ot % 2]
                    ot += 1
                    for half in range(2):
                        for kc in range(NCH):
                            cx.op(PE, lambda half=half, kc=kc: nc.tensor.matmul(
                                P[:, half * 512:(half + 1) * 512], s.t[:, kc, qi * 128:(qi + 1) * 128],
                                wd[:, kc, half * 512:(half + 1) * 512], start=(kc == 0), stop=(kc == NCH - 1)),
                                R=[s.res, r_wd[kc // 11]], W=[RP], inc=(kc == NCH - 1))
                    xo = xout.next()
                    if final:
                        dst = out_d[tix * 128:(tix + 1) * 128, :]
                    else:
                        dst = xres[tix * 128:(tix + 1) * 128, :]
                    resid_update(P, RP, xs, xo, 2 if tix < NTL else 3, dst)
            cx.barrier()

    for l in range(DEPTH):
        last = (l == DEPTH - 1)
        do_ctx = not last
        wv = w_in_d[l]
        phase_mod(l)
        with ExitStack() as hs:
            H["hT"] = sb(hs, f"hT1_{l}", [128, 8, T], BF16)
            phase_norm(1, list(range(NT)))
            phase_A(l, wv)
            phase_C(l, wv)
            phase_G(l, wv)
            phase_V(l, wv)
            phase_QK(l, wv)
        with ExitStack() as ms:
            mw = merge_weights(l, ms)
            phase_attn(l, do_ctx)
            phase_merge(l, do_ctx, mw)
        with ExitStack() as hs:
            H["hT"] = sb(hs, f"hT2_{l}", [128, 8, T], BF16)
            phase_norm(2, list(range(NT if do_ctx else NTL)))
            phase_ffn_up(l, do_ctx)
        phase_ffn_down(l, do_ctx, last)
    cx.barrier()
    es.close()
    return nc


def _rope_tables(SEQ):
    T = SEQ + CTX
    rows = SEQ // GRID_W
    row = np.repeat(np.arange(rows, dtype=np.float32), GRID_W)
    col = np.tile(np.arange(GRID_W, dtype=np.float32), rows)
    n_freq = 16
    inv = (np.float32(ROPE_THETA) ** (-np.arange(n_freq, dtype=np.float32) / np.float32(n_freq))).astype(np.float32)
    ang = np.stack([row[:, None] * inv, col[:, None] * inv], axis=1).astype(np.float32)
    cos = np.ones((T, 2, n_freq), np.float32)
    sin = np.zeros((T, 2, n_freq), np.float32)
    cos[:SEQ] = np.cos(ang)
    sin[:SEQ] = np.sin(ang)
    NT = T // 128
    cos = cos.reshape(NT, 128, 32).transpose(1, 0, 2)
    sin = sin.reshape(NT, 128, 32).transpose(1, 0, 2)
    return np.ascontiguousarray(cos), np.ascontiguousarray(sin)


def _col(v, nchunk):
    return np.ascontiguousarray(np.asarray(v, np.float32).reshape(nchunk, 128).T)


def prep_shared(inp, SEQ, DEPTH):
    f = lambda k: np.asarray(inp[k], np.float32)
    colp = np.zeros((DEPTH, 128, NCOLP), np.float32)
    rowp = np.zeros((DEPTH, 128, NROWP), np.float32)
    gbias = np.zeros((DEPTH, 128, 2, D), np.float32)
    for l in range(DEPTH):
        colp[l, :, CP_N1:CP_N1 + 8] = _col(f("norm1_g")[l], 8)
        colp[l, :, CP_N2:CP_N2 + 8] = _col(f("norm2_g")[l], 8)
        colp[l, :, CP_ADAB:CP_ADAB + 48] = _col(f("ada_b")[l], 48)
        cw = f("conv_w")[l]
        for tap in range(3):
            colp[l, :, CP_CONVW + tap * 4:CP_CONVW + tap * 4 + 4] = _col(cw[tap], 4)
        fw = f("ffn_conv_w")[l]
        for tap in range(3):
            colp[l, :, CP_FCW + tap * 44:CP_FCW + tap * 44 + 44] = _col(fw[tap], 44)
        colp[l, :, CP_FCB:CP_FCB + 44] = _col(f("ffn_conv_b")[l], 44)
        colp[l, :, CP_SGUB:CP_SGUB + 4] = f("sgu_b")[l].T
        row = np.concatenate([
            f("sgu_ln_g")[l], f("q_norm_g")[l], f("k_norm_g")[l], f("subln_g")[l],
            f("lam_q1")[l], f("lam_k1")[l], f("lam_q2")[l], f("lam_k2")[l]])
        rowp[l] = np.broadcast_to(row[None, :], (128, NROWP))
        gbias[l, :, 0, :] = f("ada_b")[l][None, 2 * D:3 * D]
        gbias[l, :, 1, :] = f("ada_b")[l][None, 5 * D:6 * D]
    sgut = np.ascontiguousarray(f("sgu_w").transpose(0, 3, 1, 2))
    cos, sin = _rope_tables(SEQ)
    shared = {
        "colp": colp, "rowp": rowp, "gbias": gbias, "sgut": sgut, "cost": cos, "sint": sin,
        "identf": np.eye(128, dtype=np.float32), "identb": np.eye(128, dtype=np.float32).astype(ml_dtypes.bfloat16),
    }
    for k in ("ada_w", "w_in", "w_br_a", "w_br_b", "w_br_c", "w_out", "ffn_up", "ffn_down"):
        shared[k] = np.ascontiguousarray(f(k))
    return shared


def core_inputs(inp, shared, b):
    m = dict(shared)
    m["x"] = np.ascontiguousarray(np.asarray(inp["x"][b], np.float32))
    m["ctx"] = np.ascontiguousarray(np.asarray(inp["ctx"][b], np.float32))
    m["ccol"] = np.ascontiguousarray(np.concatenate([_col(inp["c"][b], 8), _col(inp["c_ctx"], 8)], axis=1))
    return m


def kernel(**inputs):
    B, SEQ, _ = inputs["x"].shape
    DEPTH = inputs["w_in"].shape[0]
    nc = build(SEQ=SEQ, DEPTH=DEPTH)
    shared = prep_shared(inputs, SEQ, DEPTH)
    in_maps = [core_inputs(inputs, shared, b) for b in range(B)]
    res = run_bass_kernel_spmd(nc, in_maps, core_ids=list(range(B)))
    return np.stack([np.asarray(r["out"], np.float32) for r in res.results], axis=0)
```
